# Optimizing a Trainium2 kernel written in Bass

```python
import jax
import jax.numpy as jnp
from jax import lax
import numpy as np

D_MODEL = 1024
BATCH = 16
SEQ = 2048
DEPTH = 2

GRID_W = 64
CTX_LEN = 256

CONV_DIM = D_MODEL // 4
CONV_WIDTH = 31
NA_HEADS = 8
NA_HEAD_DIM = 64
NA_DIM = NA_HEADS * NA_HEAD_DIM
NA_ROWS_MAX = 8
NA_COLS = 16
LRU_DIM = D_MODEL // 4
LRU_BLOCKS = 4
LRU_BLOCK = LRU_DIM // LRU_BLOCKS
LRU_CONV = 4
LRU_C = 8.0
MIX_DIM = CONV_DIM + NA_DIM + LRU_DIM
D_FF = 4 * D_MODEL
EPS = 1e-6
NEG_INF = -1e30

SPLITS = [CONV_DIM, 2 * CONV_DIM, 2 * CONV_DIM + NA_DIM, 2 * CONV_DIM + 2 * NA_DIM,
          2 * CONV_DIM + 3 * NA_DIM, 2 * CONV_DIM + 3 * NA_DIM + LRU_DIM]
IN_DIM = 2 * CONV_DIM + 3 * NA_DIM + 2 * LRU_DIM
K_OFF = SPLITS[2]
LX_OFF = SPLITS[4]
LG_OFF = SPLITS[5]

kernel_name = "hybrid_conv_natten_rglru_dit"


def rmsnorm(t, g):
    tf = t.astype(jnp.float32)
    y = tf * lax.rsqrt(jnp.mean(tf * tf, axis=-1, keepdims=True) + EPS)
    return (y * g.astype(jnp.float32)).astype(t.dtype)


def layernorm(t, g, b):
    tf = t.astype(jnp.float32)
    mu = jnp.mean(tf, axis=-1, keepdims=True)
    var = jnp.mean(jnp.square(tf - mu), axis=-1, keepdims=True)
    y = (tf - mu) * lax.rsqrt(var + EPS) * g.astype(jnp.float32) + b.astype(jnp.float32)
    return y.astype(t.dtype)


def modulate(h, shift, scale):
    return h * (1 + scale) + shift


def depthwise_conv(u, w, b, pad_l, pad_r):
    out = lax.conv_general_dilated(
        u, w[:, None, :], window_strides=(1,), padding=[(pad_l, pad_r)],
        dimension_numbers=("NWC", "WIO", "NWC"), feature_group_count=u.shape[-1])
    return out + b


def conformer_conv(val, gate, w, b, ln_g, ln_b):
    u = val * jax.nn.sigmoid(gate)
    u = depthwise_conv(u, w, b, CONV_WIDTH // 2, CONV_WIDTH // 2)
    return jax.nn.silu(layernorm(u, ln_g, ln_b))


def to_heads(t):
    b, n, _ = t.shape
    return t.reshape(b, n, NA_HEADS, NA_HEAD_DIM).transpose(0, 2, 1, 3)


def na_context(q, k, v):
    b, n, _ = q.shape
    qh, kh, vh = to_heads(q), to_heads(k), to_heads(v)
    s = jnp.einsum("bhqd,bhkd->bhqk", qh, kh, preferred_element_type=jnp.float32) * NA_HEAD_DIM ** -0.5
    p = jax.nn.softmax(s, axis=-1).astype(vh.dtype)
    o = jnp.einsum("bhqk,bhkd->bhqd", p, vh)
    return o.transpose(0, 2, 1, 3).reshape(b, n, NA_DIM)


def na_latent(q, k, v, kc, vc, rpb):
    b, n, _ = q.shape
    rows = n // GRID_W
    kr = min(NA_ROWS_MAX, rows)
    grid = lambda t: t.reshape(b, rows, GRID_W, NA_HEADS, NA_HEAD_DIM).transpose(0, 3, 1, 2, 4)
    qg, kg, vg = grid(q), grid(k), grid(v)
    kch, vch = to_heads(kc), to_heads(vc)
    scale = NA_HEAD_DIM ** -0.5
    qcol = jnp.arange(GRID_W)
    col_start = jnp.clip(qcol - NA_COLS // 2, 0, GRID_W - NA_COLS)
    kcol = jnp.arange(GRID_W)
    valid = (kcol[None, :] >= col_start[:, None]) & (kcol[None, :] < col_start[:, None] + NA_COLS)
    cidx = jnp.clip(kcol[None, :] - qcol[:, None], -(NA_COLS - 1), NA_COLS - 1) + NA_COLS - 1

    def row_step(r):
        rs = jnp.clip(r - kr // 2, 0, rows - kr)
        ridx = rs + jnp.arange(kr) - r + NA_ROWS_MAX - 1
        bias = rpb[:, ridx[None, :, None], cidx[:, None, :]]
        bias = jnp.where(valid[:, None, :], bias, NEG_INF)
        qr = lax.dynamic_index_in_dim(qg, r, axis=2, keepdims=False)
        kb = lax.dynamic_slice_in_dim(kg, rs, kr, axis=2)
        vb = lax.dynamic_slice_in_dim(vg, rs, kr, axis=2)
        s_loc = jnp.einsum("bhqd,bhrkd->bhqrk", qr, kb, preferred_element_type=jnp.float32) * scale + bias
        s_loc = s_loc.reshape(b, NA_HEADS, GRID_W, kr * GRID_W)
        s_ctx = jnp.einsum("bhqd,bhcd->bhqc", qr, kch, preferred_element_type=jnp.float32) * scale
        p = jax.nn.softmax(jnp.concatenate([s_loc, s_ctx], axis=-1), axis=-1)
        p_loc = p[..., :kr * GRID_W].reshape(b, NA_HEADS, GRID_W, kr, GRID_W).astype(vb.dtype)
        p_ctx = p[..., kr * GRID_W:].astype(vch.dtype)
        return (jnp.einsum("bhqrk,bhrkd->bhqd", p_loc, vb)
                + jnp.einsum("bhqc,bhcd->bhqd", p_ctx, vch))

    out = lax.map(row_step, jnp.arange(rows))
    return out.transpose(1, 0, 3, 2, 4).reshape(b, n, NA_DIM)


def rglru_coeffs(u, wx, bx, wa, ba, lam):
    b, n, _ = u.shape
    ub = u.reshape(b, n, LRU_BLOCKS, LRU_BLOCK)
    gx = jax.nn.sigmoid(jnp.einsum("bnkd,kde->bnke", ub, wx).reshape(b, n, LRU_DIM) + bx)
    ga = jax.nn.sigmoid((jnp.einsum("bnkd,kde->bnke", ub, wa).reshape(b, n, LRU_DIM) + ba).astype(jnp.float32))
    log_a = -LRU_C * ga * jax.nn.softplus(-lam.astype(jnp.float32))
    a = jnp.exp(log_a)
    coef = jnp.sqrt(-jnp.expm1(2.0 * log_a))
    return a, coef * (gx * u).astype(jnp.float32)


def linear_scan(a, b, h0, reverse):
    def combine(e1, e2):
        a1, b1 = e1
        a2, b2 = e2
        return a1 * a2, a2 * b1 + b2
    a_cum, b_cum = lax.associative_scan(combine, (a, b), reverse=reverse, axis=1)
    return a_cum * h0[:, None, :] + b_cum


def rglru_direction(x_lat, x_ctx, conv_w, conv_b, wx, bx, wa, ba, lam, reverse):
    pad = (0, LRU_CONV - 1) if reverse else (LRU_CONV - 1, 0)
    a_c, b_c = rglru_coeffs(depthwise_conv(x_ctx, conv_w, conv_b, *pad), wx, bx, wa, ba, lam)
    h_c = linear_scan(a_c, b_c, jnp.zeros_like(b_c[:, 0]), reverse)
    h_last = h_c[:, 0] if reverse else h_c[:, -1]
    a_l, b_l = rglru_coeffs(depthwise_conv(x_lat, conv_w, conv_b, *pad), wx, bx, wa, ba, lam)
    h_l = linear_scan(a_l, b_l, h_last, reverse)
    return h_l, h_c


def sq_relu_mlp(h, w1, w2):
    return jnp.square(jax.nn.relu(h @ w1)) @ w2


def setup_inputs(seed: int = 0) -> dict:
    key = jax.random.key(seed)
    ks = jax.random.split(key, 32)
    f32 = jnp.float32

    def nrm(k, shape, scale):
        return jax.random.normal(k, shape, f32) * scale

    u = jax.random.uniform(ks[21], (DEPTH, 2, LRU_DIM), f32, 0.9, 0.999)
    a0 = u ** (1.0 / LRU_C)
    return {
        "x": nrm(ks[0], (BATCH, SEQ, D_MODEL), 1.0),
        "c": nrm(ks[1], (BATCH, D_MODEL), 1.0),
        "ctx": nrm(ks[2], (BATCH, CTX_LEN, D_MODEL), 1.0),
        "c_ctx": nrm(ks[3], (D_MODEL,), 1.0),
        "norm1_g": 1.0 + nrm(ks[4], (DEPTH, D_MODEL), 0.02),
        "norm2_g": 1.0 + nrm(ks[5], (DEPTH, D_MODEL), 0.02),
        "ada_w": nrm(ks[6], (DEPTH, D_MODEL, 6 * D_MODEL), 0.5 * D_MODEL ** -0.5),
        "ada_b": nrm(ks[7], (DEPTH, 6 * D_MODEL), 0.02),
        "w_in": nrm(ks[8], (DEPTH, D_MODEL, IN_DIM), D_MODEL ** -0.5),
        "w_out": nrm(ks[9], (DEPTH, MIX_DIM, D_MODEL), MIX_DIM ** -0.5),
        "conv_w": nrm(ks[10], (DEPTH, CONV_WIDTH, CONV_DIM), CONV_WIDTH ** -0.5),
        "conv_b": nrm(ks[11], (DEPTH, CONV_DIM), 0.02),
        "conv_ln_g": 1.0 + nrm(ks[12], (DEPTH, CONV_DIM), 0.02),
        "conv_ln_b": nrm(ks[13], (DEPTH, CONV_DIM), 0.02),
        "na_rpb": nrm(ks[14], (DEPTH, NA_HEADS, 2 * NA_ROWS_MAX - 1, 2 * NA_COLS - 1), 0.1),
        "lru_conv_w": nrm(ks[15], (DEPTH, 2, LRU_CONV, LRU_DIM), LRU_CONV ** -0.5),
        "lru_conv_b": nrm(ks[16], (DEPTH, 2, LRU_DIM), 0.02),
        "lru_wx": nrm(ks[17], (DEPTH, 2, LRU_BLOCKS, LRU_BLOCK, LRU_BLOCK), LRU_BLOCK ** -0.5),
        "lru_bx": nrm(ks[18], (DEPTH, 2, LRU_DIM), 0.02),
        "lru_wa": nrm(ks[19], (DEPTH, 2, LRU_BLOCKS, LRU_BLOCK, LRU_BLOCK), LRU_BLOCK ** -0.5),
        "lru_ba": nrm(ks[20], (DEPTH, 2, LRU_DIM), 0.02),
        "lru_lambda": jnp.log(a0) - jnp.log1p(-a0),
        "mlp_w1": nrm(ks[22], (DEPTH, D_MODEL, D_FF), D_MODEL ** -0.5),
        "mlp_w2": nrm(ks[23], (DEPTH, D_FF, D_MODEL), D_FF ** -0.5),
        "final_g": 1.0 + nrm(ks[24], (D_MODEL,), 0.02),
    }


def reference(x, c, ctx, c_ctx, norm1_g, norm2_g, ada_w, ada_b, w_in, w_out, conv_w, conv_b,
              conv_ln_g, conv_ln_b, na_rpb, lru_conv_w, lru_conv_b, lru_wx, lru_bx, lru_wa,
              lru_ba, lru_lambda, mlp_w1, mlp_w2, final_g):
    cx = ctx
    for l in range(DEPTH):
        last = l == DEPTH - 1
        mod = jax.nn.silu(c) @ ada_w[l] + ada_b[l]
        sh1, sc1, g1, sh2, sc2, g2 = jnp.split(mod[:, None, :], 6, axis=-1)
        n_cm = 2 if last else 6
        mod_c = jax.nn.silu(c_ctx) @ ada_w[l][:, :n_cm * D_MODEL] + ada_b[l][:n_cm * D_MODEL]
        mod_c = jnp.split(mod_c, n_cm)

        h = modulate(rmsnorm(x, norm1_g[l]), sh1, sc1)
        hc = modulate(rmsnorm(cx, norm1_g[l]), mod_c[0], mod_c[1])
        a_val, a_gate, q, k, v, r_x, r_gate = jnp.split(h @ w_in[l], SPLITS, axis=-1)
        if last:
            ck, cv = jnp.split(hc @ w_in[l][:, K_OFF:LX_OFF], 2, axis=-1)
            cr_x = hc @ w_in[l][:, LX_OFF:LG_OFF]
        else:
            ca_val, ca_gate, cq, ck, cv, cr_x, cr_gate = jnp.split(hc @ w_in[l], SPLITS, axis=-1)

        y_a = conformer_conv(a_val, a_gate, conv_w[l], conv_b[l], conv_ln_g[l], conv_ln_b[l])
        y_b = na_latent(q, k, v, ck, cv, na_rpb[l])
        h_f, hc_f = rglru_direction(r_x, cr_x, lru_conv_w[l, 0], lru_conv_b[l, 0], lru_wx[l, 0],
                                    lru_bx[l, 0], lru_wa[l, 0], lru_ba[l, 0], lru_lambda[l, 0], False)
        h_b, hc_b = rglru_direction(r_x, cr_x, lru_conv_w[l, 1], lru_conv_b[l, 1], lru_wx[l, 1],
                                    lru_bx[l, 1], lru_wa[l, 1], lru_ba[l, 1], lru_lambda[l, 1], True)
        y_c = jax.nn.gelu(r_gate) * (h_f + h_b).astype(r_gate.dtype)

        x = x + g1 * (jnp.concatenate([y_a, y_b, y_c], axis=-1) @ w_out[l])
        h2 = modulate(rmsnorm(x, norm2_g[l]), sh2, sc2)
        x = x + g2 * sq_relu_mlp(h2, mlp_w1[l], mlp_w2[l])

        if not last:
            yc_a = conformer_conv(ca_val, ca_gate, conv_w[l], conv_b[l], conv_ln_g[l], conv_ln_b[l])
            yc_b = na_context(cq, ck, cv)
            yc_c = jax.nn.gelu(cr_gate) * (hc_f + hc_b).astype(cr_gate.dtype)
            cx = cx + mod_c[2] * (jnp.concatenate([yc_a, yc_b, yc_c], axis=-1) @ w_out[l])
            h2c = modulate(rmsnorm(cx, norm2_g[l]), mod_c[3], mod_c[4])
            cx = cx + mod_c[5] * sq_relu_mlp(h2c, mlp_w1[l], mlp_w2[l])
    return rmsnorm(x, final_g)
```

```python
import contextlib
import numpy as np
import concourse.bass as bass
import concourse.mybir as mybir
from concourse.bass_utils import run_bass_kernel_spmd

F32 = mybir.dt.float32
BF16 = mybir.dt.bfloat16
AF = mybir.ActivationFunctionType
ALU = mybir.AluOpType

NCORES = 8
DEPTH = 2
D = 1024
KC = 8
TL = 2048
TC = 256
T = TL + TC
TILES = [(0, 512), (512, 512), (1024, 512), (1536, 512), (2048, 256)]
EPS = 1e-6
NEG = -200.0

SM = {}
_o = 0
for _n, _w in [("n1g", 8), ("n2g", 8), ("cw", 62), ("cb", 2), ("lng", 2), ("lnb", 2), ("lcw", 16),
               ("lcb", 4), ("lbx", 4), ("lba", 4), ("lam", 4), ("adab", 48)]:
    SM[_n] = _o
    _o += _w
NS = _o
CP_ID, CP_FG, CP_CT, NCP = 0, 128, 136, 160


class Op:
    __slots__ = ("eng", "fn", "deps", "dma", "slot", "semval", "tick", "signal", "prev_slot_op")


class Sched:
    ENGS = ["sp", "act", "dve", "pool", "pe"]
    NSLOT = 8

    def __init__(self, nc):
        self.nc = nc
        self.q = {e: [] for e in self.ENGS}
        self.lastw = {}
        self.readers = {}
        self.ndma = {e: 0 for e in self.ENGS}
        self.slot_last = {}
        self.sem = {}
        self.cur_fence = []
        self.fenced = set()

    def fence(self):
        ops = []
        for e in self.ENGS:
            if self.q[e]:
                ops.append(self.q[e][-1])
        for o in self.slot_last.values():
            ops.append(o)
        self.cur_fence = ops
        self.fenced = set()

    def add(self, eng, fn, r=(), w=(), dma=False):
        o = Op()
        o.eng = eng
        o.fn = fn
        o.dma = dma
        o.signal = False
        o.tick = None
        o.slot = None
        o.semval = None
        o.prev_slot_op = None
        deps = []
        for x in r:
            lw = self.lastw.get(x)
            if lw is not None:
                deps.append((lw, "raw"))
        for x in w:
            lw = self.lastw.get(x)
            if lw is not None:
                deps.append((lw, "waw"))
            for rd in self.readers.get(x, ()):
                deps.append((rd, "war"))
            if isinstance(x, str) and x.startswith("R:") and x not in self.fenced:
                self.fenced.add(x)
                for fo in self.cur_fence:
                    deps.append((fo, "raw"))
        o.deps = deps
        for x in r:
            self.readers.setdefault(x, []).append(o)
        for x in w:
            self.lastw[x] = o
            self.readers[x] = []
        if dma:
            n = self.ndma[eng]
            self.ndma[eng] = n + 1
            o.slot = n % self.NSLOT
            o.semval = 16 * (n // self.NSLOT + 1)
            key = (eng, o.slot)
            o.prev_slot_op = self.slot_last.get(key)
            self.slot_last[key] = o
        self.q[eng].append(o)
        return o

    def pe(self, fn, r=(), w=()):
        return self.add("pe", fn, r, w)

    def act(self, fn, r=(), w=()):
        return self.add("act", fn, r, w)

    def dve(self, fn, r=(), w=()):
        return self.add("dve", fn, r, w)

    def pool(self, fn, r=(), w=()):
        return self.add("pool", fn, r, w)

    def dma_sp(self, fn, r=(), w=()):
        return self.add("sp", fn, r, w, dma=True)

    def dma_pool(self, fn, r=(), w=()):
        return self.add("pool", fn, r, w, dma=True)

    @staticmethod
    def _skip(d, o):
        return (not d.dma) and (not o.dma) and d.eng == o.eng and o.eng == "pe"

    @staticmethod
    def _skip_kind(d, o, kind):
        return (not d.dma) and (not o.dma) and d.eng == o.eng and (o.eng == "pe" or kind in ("war", "waw"))

    def finalize(self):
        for e in self.ENGS:
            for o in self.q[e]:
                for (d, kind) in o.deps:
                    if d is o or d.dma:
                        continue
                    if self._skip_kind(d, o, kind):
                        continue
                    d.signal = True
        for e in self.ENGS:
            c = 0
            for o in self.q[e]:
                if o.signal and not o.dma:
                    c += 1
                    o.tick = c

    def alloc_sems(self, stack):
        nc = self.nc
        for e in self.ENGS:
            self.sem[("eng", e)] = stack.enter_context(nc.semaphore("s_" + e))
        for e in ("sp", "pool"):
            for s in range(self.NSLOT):
                self.sem[("dma", e, s)] = stack.enter_context(nc.semaphore("d_%s%d" % (e, s)))

    def emit(self, e, eng):
        seen = {}
        for o in self.q[e]:
            waits = {}
            for (d, kind) in o.deps:
                if d is o:
                    continue
                if d.dma:
                    key = ("dma", d.eng, d.slot)
                    val = d.semval
                else:
                    if self._skip_kind(d, o, kind):
                        continue
                    key = ("eng", d.eng)
                    val = d.tick
                if waits.get(key, 0) < val:
                    waits[key] = val
            if o.dma and o.prev_slot_op is not None:
                key = ("dma", e, o.slot)
                val = o.prev_slot_op.semval
                if waits.get(key, 0) < val:
                    waits[key] = val
            for key, val in waits.items():
                if seen.get(key, 0) >= val:
                    continue
                eng.wait_ge(self.sem[key], val)
                seen[key] = val
            ins = o.fn(eng)
            if o.dma:
                ins.then_inc(self.sem[("dma", e, o.slot)], 16)
            elif o.signal:
                ins.then_inc(self.sem[("eng", e)], 1)
        if e in ("sp", "pool"):
            for s in range(self.NSLOT):
                lo = self.slot_last.get((e, s))
                if lo is not None and seen.get(("dma", e, s), 0) < lo.semval:
                    eng.wait_ge(self.sem[("dma", e, s)], lo.semval)

    def run(self, stack):
        nc = self.nc
        self.finalize()
        block = stack.enter_context(nc.Block())
        S = self

        @block.sync
        def _(eng):
            S.emit("sp", eng)

        @block.scalar
        def _(eng):
            S.emit("act", eng)

        @block.vector
        def _(eng):
            S.emit("dve", eng)

        @block.gpsimd
        def _(eng):
            S.emit("pool", eng)

        @block.tensor
        def _(eng):
            S.emit("pe", eng)


def L(f, *a, **k):
    return lambda e: getattr(e, f)(*a, **k)


def MM(out, pairs, first=True, last=True):
    def fn(e):
        n = len(pairs)
        ins = None
        for i, (l, rh) in enumerate(pairs):
            ins = e.matmul(out, lhsT=l, rhs=rh, start=(first and i == 0), stop=(last and i == n - 1))
        return ins
    return fn


def _rs(r):
    return min(max(r - 4, 0), 24)


def _valid(r, k):
    return _rs(r) <= k < _rs(r) + 8


def attn_tables():
    info = {}
    runs = []
    interior_base = None
    for qt in range(16):
        r0 = 2 * qt
        kts = [kt for kt in range(16)
               if any(_valid(r0 + a, 2 * kt + b) for a in (0, 1) for b in (0, 1))]
        interior = (4 <= r0 <= 26)
        if interior:
            if interior_base is None:
                interior_base = sum(len(k) for _, k in runs)
                runs.append((r0, kts))
            info[qt] = (kts, interior_base)
        else:
            base = sum(len(k) for _, k in runs)
            runs.append((r0, kts))
            info[qt] = (kts, base)
    nb = sum(len(k) for _, k in runs)
    return info, runs, nb


ATT_INFO, ATT_RUNS, NB = attn_tables()


def build_bias_blocks(rpb_l):
    H = rpb_l.shape[0]
    out = np.full((H, 128, NB, 128), NEG, np.float32)
    p = np.arange(128)
    kp, kcol = p // 64, p % 64
    qp, qcol = p // 64, p % 64
    cstart = np.clip(qcol - 8, 0, 48)
    colvalid = (kcol[:, None] >= cstart[None, :]) & (kcol[:, None] < cstart[None, :] + 16)
    cidx = np.clip(kcol[:, None] - qcol[None, :], -15, 15) + 15
    n = 0
    for (r0, kts) in ATT_RUNS:
        for kt in kts:
            k0 = 2 * kt
            kr = k0 + kp[:, None]
            qr = r0 + qp[None, :]
            delta = kr - qr
            rowvalid = np.zeros((128, 128), bool)
            for a in (0, 1):
                for b in (0, 1):
                    if _valid(r0 + a, k0 + b):
                        rowvalid |= (qp[None, :] == a) & (kp[:, None] == b)
            ok = rowvalid & colvalid & (np.abs(delta) <= 7)
            ridx = np.clip(delta + 7, 0, 14)
            vals = rpb_l[:, ridx, cidx]
            out[:, :, n, :] = np.where(ok[None], vals, NEG)
            n += 1
    return out


def build_program(debug=False, nseq=2, nlayers=DEPTH, stop_after=None):
    nc = bass.Bass("TRN2", target_bir_lowering=False)
    x_d = nc.dram_tensor("x", [2, TL, D], F32, kind="ExternalInput").ap()
    ctx_d = nc.dram_tensor("ctx", [2, TC, D], F32, kind="ExternalInput").ap()
    cp_d = nc.dram_tensor("cpack", [128, NCP], F32, kind="ExternalInput").ap()
    sm_d = nc.dram_tensor("small", [DEPTH, 128, NS], F32, kind="ExternalInput").ap()
    adaw_d = nc.dram_tensor("ada_w", [DEPTH, D, 6 * D], F32, kind="ExternalInput").ap()
    win_d = nc.dram_tensor("w_in", [DEPTH, D, 2560], F32, kind="ExternalInput").ap()
    wout_d = nc.dram_tensor("w_out", [DEPTH, D, D], F32, kind="ExternalInput").ap()
    w1_d = nc.dram_tensor("w1", [DEPTH, D, 4 * D], F32, kind="ExternalInput").ap()
    w2_d = nc.dram_tensor("w2", [DEPTH, 4 * D, D], F32, kind="ExternalInput").ap()
    bdw_d = nc.dram_tensor("bdw", [DEPTH, 128, 8, 128], F32, kind="ExternalInput").ap()
    bb_d = nc.dram_tensor("bb", [DEPTH, 8, 128, NB, 128], F32, kind="ExternalInput").ap()
    dgw_d = nc.dram_tensor("dgw", [DEPTH, 128, 62, 128], F32, kind="ExternalInput").ap()
    dglw_d = nc.dram_tensor("dglw", [DEPTH, 128, 16, 128], F32, kind="ExternalInput").ap()
    out_d = nc.dram_tensor("out", [2, TL, D], F32, kind="ExternalOutput").ap()
    yscr = nc.dram_tensor("yscr", [128, 8, T], BF16, kind="Internal").ap()
    dbg = {}

    st = contextlib.ExitStack()
    with st:
        def sb(name, shape, dt):
            return st.enter_context(nc.sbuf_tensor(name, shape, dt))

        xT = sb("xT", [128, KC, T], F32)
        hb = sb("hb", [128, KC, T], BF16)
        W = sb("W", [128, 4, 4096], BF16)
        RC = 15872
        R = sb("R", [128, RC], F32)
        cpk = sb("cpk", [128, NCP], F32)
        smk = sb("smk", [128, DEPTH, NS], F32)
        identb = sb("identb", [128, 128], BF16)
        onesm = sb("onesm", [128, 128], BF16)
        ones256 = sb("ones256", [128, 128], BF16)
        zerosb = sb("zerosb", [128, 256], BF16)
        sct = sb("sct", [128, KC, 3], F32)
        modT = sb("modT", [128, DEPTH, 48, 3], F32)
        A1 = sb("A1", [128, DEPTH, 8, 3], F32)
        A2 = sb("A2", [128, DEPTH, 8, 3], F32)
        clt = sb("clt", [128, DEPTH, 4], F32)
        clt2 = sb("clt2", [128, DEPTH, 4], F32)
        epst = sb("epst", [128, 1], F32)
        hexp = sb("hexp", [128, 2], F32)
        lruc = sb("lruc", [128, DEPTH, 16], F32)
        P = st.enter_context(nc.psum_tensor("P", [128, 7, 512], F32))
        Pb = st.enter_context(nc.psum_tensor("Pb", [128, 1024], BF16))

        S = Sched(nc)
        S.alloc_sems(st)
        identf = cpk[:, CP_ID:CP_ID + 128]

        bank_i = [0]
        pair_i = [0]

        def bank():
            b = bank_i[0] % 7
            bank_i[0] += 1
            return b

        def bank2():
            b = 2 * (pair_i[0] % 3)
            pair_i[0] += 1
            return b

        def PSR(b):
            return ("ps", b)

        phase_n = [0]
        ar_off = [0]

        ar_top = [False, RC]

        def phase(tag, fence=True, top=False):
            phase_n[0] += 1
            ar_off[0] = 0
            ar_top[0] = top
            ar_top[1] = RC
            if fence:
                S.fence()
                return "R:%d%s:" % (phase_n[0], tag)
            return "U:%d%s:" % (phase_n[0], tag)

        def ralloc(shape, dt):
            n = 1
            for v in shape:
                n *= v
            nb = n * (2 if dt == BF16 else 4)
            ncols = (nb + 3) // 4
            if ar_top[0]:
                ar_top[1] -= ncols
                c0 = ar_top[1]
                assert c0 >= RC - 3712, ("top arena overflow", c0)
            else:
                c0 = ar_off[0]
                ar_off[0] += ncols
                assert ar_off[0] <= RC, ("arena overflow", ar_off[0])
            ap = R[:, c0:c0 + ncols]
            if dt == BF16:
                ap = ap.bitcast(BF16)
                if ncols * 2 != n:
                    ap = ap[:, 0:n]
            if len(shape) == 2:
                return ap.rearrange("p (a b) -> p a b", a=shape[0])
            if len(shape) == 3:
                return ap.rearrange("p (a b c) -> p a b c", a=shape[0], b=shape[1])
            if len(shape) == 4:
                return ap.rearrange("p (a b c d) -> p a b c d", a=shape[0], b=shape[1], c=shape[2])
            return ap

        def dump(name, ap, res):
            if not debug:
                return
            shp = list(ap.shape)
            t = nc.dram_tensor("dbg_" + name, shp, ap.dtype, kind="ExternalOutput").ap()
            dbg[name] = t
            S.dma_sp(L("dma_start", out=t, in_=ap), r=res)

        S.dma_sp(L("dma_start", out=cpk[:], in_=cp_d), w=["cpk"])
        S.dma_sp(L("dma_start", out=smk[:], in_=sm_d.rearrange("l p n -> p l n")), w=["smk"])
        S.dve(L("tensor_copy", out=identb[:], in_=identf), r=["cpk"], w=["const"])
        S.pool(L("memset", onesm[:], 1.0 / 1024.0), w=["const"])
        S.pool(L("memset", ones256[:], 1.0 / 256.0), w=["const"])
        S.pool(L("memset", zerosb[:], 0.0), w=["const"])
        S.pool(L("memset", epst[:], EPS), w=["const"])
        S.pool(L("memset", hexp[:, 0:1], -0.5), w=["const"])
        S.pool(L("memset", hexp[:, 1:2], 0.5), w=["const"])
        S.act(L("activation", out=sct[:], in_=cpk[:, CP_CT:CP_CT + 24].rearrange("p (a b) -> p a b", a=8), func=AF.Silu),
              r=["cpk"], w=["sct"])

        def smc(l, name, i=0, n=1):
            o = SM[name] + i
            return smk[:, l, o:o + n]

        def mod_steps(l, nbuf):
            pfx = phase("mod")
            awb = [ralloc([KC, 512], F32) for _ in range(nbuf)]
            modrow = ralloc([6144], F32)
            mres = "mod%d" % l
            for q in range(12):
                slot = q % nbuf
                S.dma_sp(L("dma_start", out=awb[slot], in_=adaw_d[l].rearrange("(a p) n -> p a n", p=128)[:, :, q * 512:(q + 1) * 512]),
                         w=[pfx + "aw%d" % slot])
                b = bank()
                S.pe(MM(P[0:3, b, :], [(sct[:, kc, :], awb[slot][:, kc, :]) for kc in range(KC)]), r=[pfx + "aw%d" % slot, "sct"], w=[PSR(b)])
                if q % 2 == 0:
                    S.dve(L("tensor_copy", out=modrow[0:3, q * 512:(q + 1) * 512], in_=P[0:3, b, :]), r=[PSR(b)], w=[pfx + "modrow"])
                else:
                    S.act(L("activation", out=modrow[0:3, q * 512:(q + 1) * 512], in_=P[0:3, b, :], func=AF.Copy), r=[PSR(b)], w=[pfx + "modrow"])
                yield q
            b = bank()
            mp = P[:, b, 0:144]

            def fnT(e, mp=mp):
                ins = None
                for j in range(48):
                    ins = e.transpose(mp[:, j * 3:(j + 1) * 3], modrow[0:3, j * 128:(j + 1) * 128], identf[0:3, 0:3])
                return ins
            S.pe(fnT, r=[pfx + "modrow", "cpk"], w=[PSR(b)])
            S.dve(L("tensor_tensor", out=modT[:, l, :, :], in0=mp.rearrange("p (a b) -> p a b", b=3),
                    in1=smc(l, "adab", 0, 48).unsqueeze(2).to_broadcast([128, 48, 3]), op=ALU.add),
                  r=[PSR(b), "smk"], w=[mres])
            for (A, gname, j0) in ((A1, "n1g", 8), (A2, "n2g", 32)):
                S.dve(L("scalar_tensor_tensor", out=A[:, l, :, :], in0=modT[:, l, j0:j0 + 8, :], scalar=1.0,
                        in1=smc(l, gname, 0, 8).unsqueeze(2).to_broadcast([128, 8, 3]), op0=ALU.add, op1=ALU.mult),
                      r=[mres, "smk"], w=[mres])
            cres = "clt%d" % l
            S.act(L("activation", out=clt[:, l, :], in_=smc(l, "lam", 0, 4), func=AF.Exp, scale=-1.0), r=["smk"], w=[cres])
            S.act(L("activation", out=clt[:, l, :], in_=clt[:, l, :], func=AF.Ln, bias=1.0), r=[cres], w=[cres])
            S.dve(L("tensor_scalar", out=lruc[:, l, 0:4], in0=clt[:, l, :], scalar1=-8.0, scalar2=None, op0=ALU.mult), r=[cres], w=[cres + "b"])
            S.dve(L("tensor_scalar", out=lruc[:, l, 4:8], in0=clt[:, l, :], scalar1=-4.0, scalar2=None, op0=ALU.mult), r=[cres], w=[cres + "b"])
            S.dve(L("tensor_scalar", out=lruc[:, l, 8:12], in0=smc(l, "lbx", 0, 4), scalar1=0.5, scalar2=None, op0=ALU.mult), r=["smk"], w=[cres + "b"])
            S.dve(L("tensor_scalar", out=lruc[:, l, 12:16], in0=smc(l, "lba", 0, 4), scalar1=0.5, scalar2=None, op0=ALU.mult), r=["smk"], w=[cres + "b"])
            yield 99

        for _ in mod_steps(0, 2):
            pass
        deferred_mod = {"gen": None}
        if debug:
            dump("modT", modT[:], ["mod0"])

        def dump_y():
            if not debug:
                return
            pfx = phase("dbg")
            t_ = nc.dram_tensor("dbg_y", [128, 8, T], BF16, kind="ExternalOutput").ap()
            dbg["y"] = t_
            for t in range(5):
                o, n = TILES[t]
                buf = ralloc([8, 512], BF16)
                S.dma_sp(L("dma_start", out=buf[:, :, 0:n], in_=yscr[:, :, o:o + n]), r=[("yscr", t)], w=[pfx + "b%d" % t])
                S.dma_sp(L("dma_start", out=t_[:, :, o:o + n], in_=buf[:, :, 0:n]), r=[pfx + "b%d" % t])

        def SH1(l, kc, col): return modT[:, l, 0 + kc, col:col + 1]
        def G1(l, kc, col): return modT[:, l, 16 + kc, col:col + 1]
        def SH2(l, kc, col): return modT[:, l, 24 + kc, col:col + 1]
        def G2(l, kc, col): return modT[:, l, 40 + kc, col:col + 1]

        def colof(s, t):
            return 2 if t == 4 else s

        def load_x(s):
            pfx = phase("ld")
            xin = [ralloc([1024], F32) for _ in range(2)]
            for i in range(18):
                src = x_d[s, i * 128:(i + 1) * 128, :] if i < 16 else ctx_d[s, (i - 16) * 128:(i - 15) * 128, :]
                slot = i % 2
                t = min(i // 4, 4)
                S.dma_sp(L("dma_start", out=xin[slot], in_=src), w=[pfx + "xin%d" % slot])
                for half in range(2):
                    b = bank()

                    def fn(e, b=b, half=half, slot=slot):
                        ins = None
                        for q in range(4):
                            kc = half * 4 + q
                            ins = e.transpose(P[:, b, q * 128:(q + 1) * 128], xin[slot][:, kc * 128:(kc + 1) * 128], identf)
                        return ins
                    S.pe(fn, r=[pfx + "xin%d" % slot, "cpk"], w=[PSR(b)])
                    dst = xT[:, half * 4:half * 4 + 4, i * 128:(i + 1) * 128]
                    src_ps = P[:, b, :].rearrange("p (a n) -> p a n", a=4)
                    if half == 0:
                        S.dve(L("tensor_copy", out=dst, in_=src_ps), r=[PSR(b)], w=[("x", t)])
                    else:
                        S.act(L("activation", out=dst, in_=src_ps, func=AF.Copy), r=[PSR(b)], w=[("x", t)])

        def rms_stats(pfx, t, sq, sd, rstd, tag=""):
            o, n = TILES[t]
            S.act(L("activation", out=sq[:, :, 0:n], in_=xT[:, :, o:o + n], func=AF.Square), r=[("x", t)], w=[pfx + "sq" + tag])
            b = bank()
            S.pe(MM(P[:, b, 0:n], [(onesm[:], sq[:, kc, 0:n]) for kc in range(KC)]), r=[pfx + "sq" + tag, "const"], w=[PSR(b)])
            S.act(L("activation", out=sd[:, 0:n], in_=P[:, b, 0:n], func=AF.Ln, bias=epst[:, 0:1]), r=[PSR(b), "const"], w=[pfx + "sd" + tag])
            S.act(L("activation", out=rstd[:, 0:n], in_=sd[:, 0:n], func=AF.Exp, scale=-0.5), r=[pfx + "sd" + tag], w=[pfx + "rstd" + tag])

        def norm_phase(s, l, which, tiles, fence=True):
            pfx = phase("nrm", fence=fence)
            sq = [ralloc([KC, 512], BF16) for _ in range(2)]
            sd = [ralloc([512], F32) for _ in range(2)]
            rstd = [ralloc([512], F32) for _ in range(2)]
            tmp = [ralloc([4, 512], F32) for _ in range(2)]
            assert ar_off[0] <= RC - 3712
            A = A1 if which == 1 else A2
            SH = SH1 if which == 1 else SH2
            hc_ = 0
            for ti, t in enumerate(tiles):
                o, n = TILES[t]
                col = colof(s, t)
                k = ti % 2
                tg = "%d" % k
                rms_stats(pfx, t, sq[k], sd[k], rstd[k], tg)
                for half in range(2):
                    k2 = hc_ % 2
                    hc_ += 1
                    tg2 = "%d" % k2
                    S.dve(L("tensor_tensor", out=tmp[k2][:, :, 0:n], in0=xT[:, half * 4:half * 4 + 4, o:o + n],
                            in1=rstd[k][:, 0:n].unsqueeze(1).to_broadcast([128, 4, n]), op=ALU.mult),
                          r=[("x", t), pfx + "rstd" + tg], w=[pfx + "tmp" + tg2])
                    for q in range(4):
                        kc = half * 4 + q
                        if kc % 2 == 0:
                            S.act(L("activation", out=hb[:, kc, o:o + n], in_=tmp[k2][:, q, 0:n], func=AF.Identity,
                                    scale=A[:, l, kc, col:col + 1], bias=SH(l, kc, col)),
                                  r=[pfx + "tmp" + tg2, "mod%d" % l], w=[("h", t)])
                        else:
                            S.dve(L("tensor_scalar", out=hb[:, kc, o:o + n], in0=tmp[k2][:, q, 0:n],
                                    scalar1=A[:, l, kc, col:col + 1], scalar2=SH(l, kc, col), op0=ALU.mult, op1=ALU.add),
                                  r=[pfx + "tmp" + tg2, "mod%d" % l], w=[("h", t)])

        def load_w512(l, slot, c0):
            dst = W[:, slot, :].rearrange("p (a b) -> p a b", a=KC)
            S.dma_pool(L("dma_start", out=dst, in_=win_d[l].rearrange("(a p) n -> p a n", p=128)[:, :, c0:c0 + 512]),
                       w=[("W", slot)])
            return dst

        ystage_i = [0]

        def conv_phase(s, l, tiles):
            pfx = phase("cv")
            wA = load_w512(l, 0, 0)
            ub = ralloc([2, 2364], BF16)
            dg = ralloc([2, 31, 128], BF16)
            sig = [ralloc([512], F32) for _ in range(2)]
            cv = [ralloc([2, 512], F32) for _ in range(2)]
            cvb = ralloc([2, 512], BF16)
            sqc = ralloc([2, 512], BF16)
            mean = ralloc([512], F32)
            m2 = ralloc([512], F32)
            sd = ralloc([512], F32)
            rstd = ralloc([512], F32)
            dd = ralloc([2, 512], F32)
            ys = [ralloc([2, 512], BF16) for _ in range(2)]
            S.dma_pool(L("dma_start", out=dg.rearrange("p a b c -> p (a b) c"), in_=dgw_d[l]), w=[pfx + "dg"])
            for (a0, a1) in ((0, 15), (2063, 2093), (2349, 2364)):
                S.pool(L("memset", ub[:, :, a0:a1], 0.0), w=[pfx + "ub"])

            def upos(t):
                o, n = TILES[t]
                return (15 + o) if t < 4 else (2078 + 15)

            for t in tiles:
                o, n = TILES[t]
                bs = [bank() for _ in range(4)]
                for j in range(4):
                    S.pe(MM(P[:, bs[j], 0:n], [(wA[:, kc, j * 128:(j + 1) * 128], hb[:, kc, o:o + n]) for kc in range(KC)]),
                         r=[("W", 0), ("h", t)], w=[PSR(bs[j])])
                for c in range(2):
                    S.act(L("activation", out=sig[c][:, 0:n], in_=P[:, bs[2 + c], 0:n], func=AF.Tanh, scale=0.5),
                          r=[PSR(bs[2 + c])], w=[pfx + "sig%d" % c])
                    S.dve(L("scalar_tensor_tensor", out=ub[:, c, upos(t):upos(t) + n], in0=sig[c][:, 0:n], scalar=1.0, in1=P[:, bs[c], 0:n],
                            op0=ALU.add, op1=ALU.mult),
                          r=[PSR(bs[c]), pfx + "sig%d" % c], w=[pfx + "ub"])
            for ti, t in enumerate(tiles):
                o, n = TILES[t]
                base = o if t < 4 else 2078
                bs = [bank() for _ in range(2)]
                k2 = ti % 2
                cvk = cv[k2]
                cvr = pfx + "cv%d" % k2
                for c in range(2):
                    S.pe(MM(P[:, bs[c], 0:n], [(dg[:, c, k, :], ub[:, c, base + k:base + k + n]) for k in range(31)]),
                         r=[pfx + "dg", pfx + "ub"], w=[PSR(bs[c])])
                    S.dve(L("tensor_scalar", out=cvk[:, c, 0:n], in0=P[:, bs[c], 0:n], scalar1=0.5, scalar2=smc(l, "cb", c, 1), op0=ALU.mult, op1=ALU.add),
                          r=[PSR(bs[c]), "smk"], w=[cvr])
                S.act(L("activation", out=cvb[:, :, 0:n], in_=cvk[:, :, 0:n], func=AF.Copy), r=[cvr], w=[pfx + "cvb"])
                S.act(L("activation", out=sqc[:, :, 0:n], in_=cvk[:, :, 0:n], func=AF.Square), r=[cvr], w=[pfx + "sqc"])
                b1 = bank()
                b2 = bank()
                S.pe(MM(P[:, b1, 0:n], [(ones256[:], cvb[:, c, 0:n]) for c in range(2)]), r=["const", pfx + "cvb"], w=[PSR(b1)])
                S.pe(MM(P[:, b2, 0:n], [(ones256[:], sqc[:, c, 0:n]) for c in range(2)]), r=["const", pfx + "sqc"], w=[PSR(b2)])
                S.act(L("activation", out=mean[:, 0:n], in_=P[:, b1, 0:n], func=AF.Copy), r=[PSR(b1)], w=[pfx + "mean"])
                S.act(L("activation", out=m2[:, 0:n], in_=mean[:, 0:n], func=AF.Square), r=[pfx + "mean"], w=[pfx + "m2"])
                S.act(L("activation", out=sd[:, 0:n], in_=P[:, b2, 0:n], func=AF.Identity, bias=epst[:, 0:1]), r=[PSR(b2), "const"], w=[pfx + "sd"])
                S.dve(L("tensor_tensor", out=sd[:, 0:n], in0=sd[:, 0:n], in1=m2[:, 0:n], op=ALU.subtract),
                      r=[pfx + "sd", pfx + "m2"], w=[pfx + "sd"])
                S.act(L("activation", out=sd[:, 0:n], in_=sd[:, 0:n], func=AF.Ln), r=[pfx + "sd"], w=[pfx + "sd"])
                S.act(L("activation", out=rstd[:, 0:n], in_=sd[:, 0:n], func=AF.Exp, scale=-0.5), r=[pfx + "sd"], w=[pfx + "rstd"])
                S.pool(L("tensor_tensor", out=dd[:, :, 0:n], in0=cvk[:, :, 0:n],
                         in1=mean[:, 0:n].unsqueeze(1).to_broadcast([128, 2, n]), op=ALU.subtract),
                       r=[cvr, pfx + "mean"], w=[pfx + "dd"])
                S.dve(L("tensor_tensor", out=dd[:, :, 0:n], in0=dd[:, :, 0:n],
                        in1=rstd[:, 0:n].unsqueeze(1).to_broadcast([128, 2, n]), op=ALU.mult),
                      r=[pfx + "dd", pfx + "rstd"], w=[pfx + "dd"])
                ysl = ti % 2
                for c in range(2):
                    S.act(L("activation", out=ys[ysl][:, c, 0:n], in_=dd[:, c, 0:n], func=AF.Silu,
                            scale=smc(l, "lng", c, 1), bias=smc(l, "lnb", c, 1)),
                          r=[pfx + "dd", "smk"], w=[pfx + "ys%d" % ysl])
                S.dma_sp(L("dma_start", out=yscr[:, 0:2, o:o + n], in_=ys[ysl][:, :, 0:n]), r=[pfx + "ys%d" % ysl], w=[("yscr", t)])

        def lru_phase(s, l, out_tiles):
            pfx = phase("lru")
            wC = load_w512(l, 1, 2048)
            bd = ralloc([8, 128], BF16)
            dgl = ralloc([16, 128], BF16)
            rxb = ralloc([2316], BF16)
            gg = ralloc([T], BF16)
            hsum = ralloc([T], F32)
            tset = []
            for d in range(2):
                tset.append(dict(ucf=ralloc([512], F32), ucb=ralloc([512], BF16), gx=ralloc([512], F32), ga=ralloc([512], F32),
                                 a2=ralloc([512], F32), bt=ralloc([512], F32), hs=[ralloc([512], F32) for _ in range(2)]))
            t1 = ralloc([512], F32)
            ys = [ralloc([512], BF16) for _ in range(2)]
            S.dma_pool(L("dma_start", out=bd, in_=bdw_d[l]), w=[pfx + "bd"])
            S.dma_pool(L("dma_start", out=dgl, in_=dglw_d[l]), w=[pfx + "dgl"])

            def rpos(t):
                o, n = TILES[t]
                return (3 + o) if t < 4 else (2054 + 3)

            order = ([4, 0, 1, 2, 3], [4, 3, 2, 1, 0])
            step_of = [{t: i for i, t in enumerate(order[d])} for d in range(2)]
            ysi = [0]
            for c in range(2):
                cres = "c%d" % c
                for (a0, a1) in ((0, 3), (2051, 2057), (2313, 2316)):
                    S.pool(L("memset", rxb[:, a0:a1], 0.0), w=[pfx + "rxb"])
                for t in range(5):
                    o, n = TILES[t]
                    b = bank()
                    S.pe(MM(P[:, b, 0:n], [(wC[:, kc, c * 128:(c + 1) * 128], hb[:, kc, o:o + n]) for kc in range(KC)]),
                         r=[("W", 1), ("h", t)], w=[PSR(b)])
                    S.dve(L("tensor_copy", out=rxb[:, rpos(t):rpos(t) + n], in_=P[:, b, 0:n]), r=[PSR(b)], w=[pfx + "rxb"])
                    if t in out_tiles:
                        b = bank()
                        S.pe(MM(P[:, b, 0:n], [(wC[:, kc, (2 + c) * 128:(3 + c) * 128], hb[:, kc, o:o + n]) for kc in range(KC)]),
                             r=[("W", 1), ("h", t)], w=[PSR(b)])
                        S.act(L("activation", out=gg[:, o:o + n], in_=P[:, b, 0:n], func=AF.Gelu_apprx_tanh),
                              r=[PSR(b)], w=[pfx + "gg"])
                prev = [None, None]
                prev_res = [None, None]
                for step in range(5):
                    for d in range(2):
                        t = order[d][step]
                        o, n = TILES[t]
                        pi = d * 2 + c
                        ts_ = tset[d]
                        dn = "%d" % d
                        first = (step_of[d][t] < step_of[1 - d][t]) or (step_of[d][t] == step_of[1 - d][t] and d == 0)
                        base = (o if t < 4 else 2054) + (0 if d == 0 else 3)
                        b = bank()
                        S.pe(MM(P[:, b, 0:n], [(dgl[:, pi * 4 + k, :], rxb[:, base + k:base + k + n]) for k in range(4)]),
                             r=[pfx + "dgl", pfx + "rxb"], w=[PSR(b)])
                        S.act(L("activation", out=ts_["ucf"][:, 0:n], in_=P[:, b, 0:n], func=AF.Identity, bias=smc(l, "lcb", pi, 1)),
                              r=[PSR(b), "smk"], w=[pfx + "ucf" + dn])
                        S.act(L("activation", out=ts_["ucb"][:, 0:n], in_=ts_["ucf"][:, 0:n], func=AF.Copy),
                              r=[pfx + "ucf" + dn], w=[pfx + "ucb" + dn])
                        bx_ = bank()
                        ba_ = bank()
                        S.pe(MM(P[:, bx_, 0:n], [(bd[:, (d * 2 + 0) * 2 + c, :], ts_["ucb"][:, 0:n])]), r=[pfx + "bd", pfx + "ucb" + dn], w=[PSR(bx_)])
                        S.pe(MM(P[:, ba_, 0:n], [(bd[:, (d * 2 + 1) * 2 + c, :], ts_["ucb"][:, 0:n])]), r=[pfx + "bd", pfx + "ucb" + dn], w=[PSR(ba_)])
                        lc = lruc[:, l, :]
                        S.act(L("activation", out=ts_["gx"][:, 0:n], in_=P[:, bx_, 0:n], func=AF.Tanh, scale=0.5, bias=lc[:, 8 + pi:9 + pi]),
                              r=[PSR(bx_), "clt%db" % l], w=[pfx + "gx" + dn])
                        S.act(L("activation", out=ts_["ga"][:, 0:n], in_=P[:, ba_, 0:n], func=AF.Tanh, scale=0.5, bias=lc[:, 12 + pi:13 + pi]),
                              r=[PSR(ba_), "clt%db" % l], w=[pfx + "ga" + dn])
                        S.act(L("activation", out=ts_["a2"][:, 0:n], in_=ts_["ga"][:, 0:n], func=AF.Exp, scale=lc[:, pi:pi + 1], bias=lc[:, pi:pi + 1]),
                              r=[pfx + "ga" + dn, "clt%db" % l], w=[pfx + "a2" + dn])
                        S.act(L("activation", out=ts_["ga"][:, 0:n], in_=ts_["ga"][:, 0:n], func=AF.Exp, scale=lc[:, 4 + pi:5 + pi], bias=lc[:, 4 + pi:5 + pi]),
                              r=[pfx + "ga" + dn, "clt%db" % l], w=[pfx + "ga" + dn])
                    for d in range(2):
                        t = order[d][step]
                        o, n = TILES[t]
                        pi = d * 2 + c
                        ts_ = tset[d]
                        dn = "%d" % d
                        lc = lruc[:, l, :]
                        first = (step_of[d][t] < step_of[1 - d][t]) or (step_of[d][t] == step_of[1 - d][t] and d == 0)
                        S.act(L("activation", out=ts_["a2"][:, 0:n], in_=ts_["a2"][:, 0:n], func=AF.Sqrt, scale=-1.0, bias=1.0),
                              r=[pfx + "a2" + dn], w=[pfx + "a2" + dn])
                        S.dve(L("scalar_tensor_tensor", out=ts_["bt"][:, 0:n], in0=ts_["gx"][:, 0:n], scalar=1.0, in1=ts_["ucf"][:, 0:n],
                                op0=ALU.add, op1=ALU.mult),
                              r=[pfx + "gx" + dn, pfx + "ucf" + dn], w=[pfx + "bt" + dn])
                        S.dve(L("scalar_tensor_tensor", out=ts_["bt"][:, 0:n], in0=ts_["bt"][:, 0:n], scalar=0.5, in1=ts_["a2"][:, 0:n],
                                op0=ALU.mult, op1=ALU.mult),
                              r=[pfx + "bt" + dn, pfx + "a2" + dn], w=[pfx + "bt" + dn])
                        need_out = t in out_tiles
                        if first and need_out:
                            dst = hsum[:, o:o + n]
                            dres = pfx + "hsum%d" % t
                        else:
                            hsl = step % 2
                            dst = ts_["hs"][hsl][:, 0:n]
                            dres = pfx + "hs%d_%d" % (d, hsl)
                        init = 0.0 if prev[d] is None else prev[d]
                        rr = [pfx + "ga" + dn, pfx + "bt" + dn] + ([] if prev[d] is None else [prev_res[d]])
                        if d == 0:
                            S.dve(L("tensor_tensor_scan", out=dst, data0=ts_["ga"][:, 0:n], data1=ts_["bt"][:, 0:n], initial=init,
                                    op0=ALU.mult, op1=ALU.add), r=rr, w=[dres])
                            prev[d] = dst[:, n - 1:n]
                        else:
                            S.dve(L("tensor_tensor_scan", out=dst[:, ::-1], data0=ts_["ga"][:, 0:n][:, ::-1], data1=ts_["bt"][:, 0:n][:, ::-1],
                                    initial=init, op0=ALU.mult, op1=ALU.add), r=rr, w=[dres])
                            prev[d] = dst[:, 0:1]
                        prev_res[d] = dres
                        if need_out and not first:
                            S.dve(L("tensor_tensor", out=t1[:, 0:n], in0=dst, in1=hsum[:, o:o + n], op=ALU.add),
                                  r=[dres, pfx + "hsum%d" % t], w=[pfx + "t1"])
                            ysl = ysi[0] % 2
                            ysi[0] += 1
                            S.dve(L("tensor_tensor", out=ys[ysl][:, 0:n], in0=t1[:, 0:n], in1=gg[:, o:o + n], op=ALU.mult),
                                  r=[pfx + "t1", pfx + "gg"], w=[pfx + "ys%d" % ysl])
                            S.dma_sp(L("dma_start", out=yscr[:, 6 + c, o:o + n], in_=ys[ysl][:, 0:n]), r=[pfx + "ys%d" % ysl], w=[("yscr", t)])

        def attn_phase(s, l, do_ctx_queries):
            pfx = phase("att")
            wV = load_w512(l, 2, 1536)
            qk = [ralloc([2, T], BF16) for _ in range(2)]
            vx = ralloc([18, 8, 65], BF16)
            EB = [ralloc([NB, 128], BF16) for _ in range(2)]
            PT = [ralloc([1024], BF16) for _ in range(4)]
            ytok = ralloc([18, 128], BF16)
            yst = [ralloc([512], BF16) for _ in range(2)]
            rden = [ralloc([1], F32) for _ in range(4)]
            S.pool(L("memset", vx[:, :, :, 64:65], 1.0), w=[pfx + "vx1"])
            for i in range(18):
                b = bank()
                S.pe(MM(P[:, b, :], [(hb[:, kc, i * 128:(i + 1) * 128], wV[:, kc, :]) for kc in range(KC)]),
                     r=[("W", 2), ("h", min(i // 4, 4))], w=[PSR(b)])
                src = P[:, b, :].rearrange("p (a n) -> p a n", a=8)
                if i % 2 == 0:
                    S.dve(L("tensor_copy", out=vx[:, i, :, 0:64], in_=src), r=[PSR(b)], w=[pfx + "vx"])
                else:
                    S.act(L("activation", out=vx[:, i, :, 0:64], in_=src, func=AF.Copy), r=[PSR(b)], w=[pfx + "vx"])
            acnt = [0]
            ocnt = [0]
            nqt = 18 if do_ctx_queries else 16
            items = []
            for qt in range(nqt):
                if qt < 16:
                    kts, base = ATT_INFO[qt]
                    items.append((qt, list(kts) + [16, 17], len(kts), base))
                else:
                    items.append((qt, [16, 17], 0, 0))

            def emit_proj(hp):
                sl = hp % 2
                wq = W[:, 3, 0:1024].rearrange("p (a b) -> p a b", a=KC)
                wk = W[:, 3, 1024:2048].rearrange("p (a b) -> p a b", a=KC)
                if hp % 2 == 1:
                    wq = W[:, 3, 2048:3072].rearrange("p (a b) -> p a b", a=KC)
                    wk = W[:, 3, 3072:4096].rearrange("p (a b) -> p a b", a=KC)
                wres = ("W3", hp % 2)
                wsrc = win_d[l].rearrange("(a p) n -> p a n", p=128)
                S.dma_pool(L("dma_start", out=wq, in_=wsrc[:, :, 512 + hp * 128:512 + (hp + 1) * 128]), w=[wres, ("W", 3)])
                S.dma_pool(L("dma_start", out=wk, in_=wsrc[:, :, 1024 + hp * 128:1024 + (hp + 1) * 128]), w=[wres, ("W", 3)])
                qres = pfx + "qk%d" % sl
                for t in range(5):
                    o, n = TILES[t]
                    bq = bank()
                    bk = bank()
                    if t < 4 or do_ctx_queries:
                        S.pe(MM(P[:, bq, 0:n], [(wq[:, kc, :], hb[:, kc, o:o + n]) for kc in range(KC)]), r=[wres, ("W", 3), ("h", t)], w=[PSR(bq)])
                        S.act(L("activation", out=qk[sl][:, 0, o:o + n], in_=P[:, bq, 0:n], func=AF.Copy, scale=0.125), r=[PSR(bq)], w=[qres])
                    S.pe(MM(P[:, bk, 0:n], [(wk[:, kc, :], hb[:, kc, o:o + n]) for kc in range(KC)]), r=[wres, ("W", 3), ("h", t)], w=[PSR(bk)])
                    S.dve(L("tensor_copy", out=qk[sl][:, 1, o:o + n], in_=P[:, bk, 0:n]), r=[PSR(bk)], w=[qres])

            def emit_EB(h):
                esl = h % 2
                eres = pfx + "EB%d" % esl
                S.dma_pool(L("dma_start", out=EB[esl], in_=bb_d[l, h]), w=[eres])
                S.act(L("activation", out=EB[esl], in_=EB[esl], func=AF.Exp), r=[eres], w=[eres])

            def stage1(hp, hh, it):
                h = hp * 2 + hh
                hr = slice(hh * 64, hh * 64 + 64)
                esl = h % 2
                eres = pfx + "EB%d" % esl
                sl = hp % 2
                qres = pfx + "qk%d" % sl
                qt, klist, nl, base = it
                nk = len(klist)
                b2 = 2 * (acnt[0] % 2)
                psl = acnt[0] % 4
                acnt[0] += 1
                ps = P[:, b2:b2 + 2, :].rearrange("p b n -> p (b n)")

                def fn(e, ps=ps, klist=klist, qt=qt, sl=sl, hr=hr):
                    ins = None
                    for i, kt in enumerate(klist):
                        ins = e.matmul(ps[:, i * 128:(i + 1) * 128], lhsT=qk[sl][hr, 1, kt * 128:(kt + 1) * 128],
                                       rhs=qk[sl][hr, 0, qt * 128:(qt + 1) * 128], start=True, stop=True)
                    return ins
                S.pe(fn, r=[qres], w=[PSR(b2), PSR(b2 + 1)])
                pres = pfx + "PT%d" % psl
                S.act(L("activation", out=PT[psl][:, 0:nk * 128], in_=ps[:, 0:nk * 128], func=AF.Exp),
                      r=[PSR(b2), PSR(b2 + 1)], w=[pres])
                if nl > 0:
                    S.dve(L("tensor_tensor", out=PT[psl][:, 0:nl * 128], in0=PT[psl][:, 0:nl * 128],
                            in1=EB[esl][:, base:base + nl, :].rearrange("p a b -> p (a b)"), op=ALU.mult),
                          r=[pres, eres], w=[pres])
                return psl

            def stage2(hp, hh, it, psl):
                h = hp * 2 + hh
                qt, klist, nl, base = it
                pres = pfx + "PT%d" % psl
                bo = 4 + (ocnt[0] % 3)
                rsl = ocnt[0] % 4
                ocnt[0] += 1
                po = P[:, bo, 0:65]
                S.pe(MM(po, [(PT[psl][:, i * 128:(i + 1) * 128], vx[:, kt, h, :]) for i, kt in enumerate(klist)]),
                     r=[pres, pfx + "vx", pfx + "vx1"], w=[PSR(bo)])
                S.dve(L("reciprocal", out=rden[rsl], in_=P[:, bo, 64:65]), r=[PSR(bo)], w=[pfx + "rden%d" % rsl])
                S.dve(L("tensor_scalar", out=ytok[:, qt, hh * 64:(hh + 1) * 64], in0=P[:, bo, 0:64], scalar1=rden[rsl][:, 0:1], scalar2=None,
                        op0=ALU.mult),
                      r=[PSR(bo), pfx + "rden%d" % rsl], w=[pfx + "ytok%d_%d" % (qt, hh)])
                if hh == 1 and (qt % 4 == 3 or qt == nqt - 1):
                    qts = list(range(4 * (qt // 4), qt + 1))
                    nq = len(qts)

                    def fnT(e, qts=qts):
                        ins = None
                        for qi, q_ in enumerate(qts):
                            ins = e.transpose(Pb[:, qi * 128:(qi + 1) * 128], ytok[:, q_, :], identb[:])
                        return ins
                    S.pe(fnT, r=[pfx + "ytok%d_%d" % (q_, a) for q_ in qts for a in range(2)] + ["const"], w=["psb"])
                    ysl = ystage_i[0] % 2
                    ystage_i[0] += 1
                    S.dve(L("tensor_copy", out=yst[ysl][:, 0:nq * 128], in_=Pb[:, 0:nq * 128]), r=["psb"], w=[pfx + "yst%d" % ysl])
                    o0 = qts[0] * 128
                    S.dma_sp(L("dma_start", out=yscr[:, 2 + hp, o0:o0 + nq * 128], in_=yst[ysl][:, 0:nq * 128]),
                             r=[pfx + "yst%d" % ysl], w=[("yscr", min(qts[0] // 4, 4))])

            LAG = 3
            pend = []
            emit_proj(0)
            emit_EB(0)
            for hp in range(4):
                for hh in range(2):
                    h = hp * 2 + hh
                    for ii, it in enumerate(items):
                        if ii == 6 and h + 1 < 8:
                            emit_EB(h + 1)
                        if hh == 1 and ii == 10 and hp + 1 < 4:
                            emit_proj(hp + 1)
                        psl = stage1(hp, hh, it)
                        pend.append((hp, hh, it, psl))
                        if len(pend) > LAG:
                            stage2(*pend.pop(0))
            while pend:
                stage2(*pend.pop(0))

        def wout_phase(s, l, tiles):
            pfx = phase("wo")
            yin = [ralloc([KC, 512], BF16) for _ in range(2)]
            wsrc = wout_d[l].rearrange("(a p) n -> p a n", p=128)
            wo = []
            for hf in range(2):
                dst = W[:, hf, :].rearrange("p (a b) -> p a b", a=KC)
                S.dma_pool(L("dma_start", out=dst, in_=wsrc[:, :, hf * 512:(hf + 1) * 512]), w=[("W", hf)])
                wo.append(dst)
            for t in tiles:
                o, n = TILES[t]
                col = colof(s, t)
                sl = t % 2
                S.dma_sp(L("dma_start", out=yin[sl][:, :, 0:n], in_=yscr[:, :, o:o + n]), r=[("yscr", t)], w=[pfx + "yin%d" % sl])
                for c in range(KC):
                    b = bank()
                    S.pe(MM(P[:, b, 0:n], [(wo[c // 4][:, k, (c % 4) * 128:(c % 4 + 1) * 128], yin[sl][:, k, 0:n]) for k in range(KC)]),
                         r=[("W", c // 4), pfx + "yin%d" % sl], w=[PSR(b)])
                    S.dve(L("scalar_tensor_tensor", out=xT[:, c, o:o + n], in0=P[:, b, 0:n], scalar=G1(l, c, col), in1=xT[:, c, o:o + n],
                            op0=ALU.mult, op1=ALU.add), r=[PSR(b), "mod%d" % l, ("x", t)], w=[("x", t)])

        NG = 8

        def mlp_prefetch(l, g):
            sa = 2 * ((g + 1) % 2)
            s1, s2 = sa, sa + 1
            w1src = w1_d[l].rearrange("(a p) n -> p a n", p=128)
            w2src = w2_d[l].rearrange("(a p) n -> p a n", p=128)
            w1g = W[:, s1, :].rearrange("p (a b) -> p a b", a=KC)
            w2g = W[:, s2, :].rearrange("p (a b) -> p a b", a=4)
            S.dma_pool(L("dma_start", out=w1g, in_=w1src[:, :, g * 512:(g + 1) * 512]), w=[("W", s1)])
            S.dma_pool(L("dma_start", out=w2g, in_=w2src[:, g * 4:(g + 1) * 4, :]), w=[("W", s2)])
            return w1g, w2g, s1, s2

        def mlp_phase(s, l, tiles, pre):
            pfx = phase("mlp", fence=False, top=True)
            hid = [ralloc([4, 512], BF16) for _ in range(2)]
            rl = [ralloc([512], F32) for _ in range(3)]
            cnt = {"h": 0, "r": 0}
            wts = {0: pre}
            if deferred_mod["gen"] == "start":
                deferred_mod["gen"] = mod_steps(1, 1)

            def stageA(g, t):
                w1g, w2g, s1, s2 = wts[g]
                o, n = TILES[t]
                hsl = cnt["h"] % 2
                cnt["h"] += 1
                hres = pfx + "hid%d" % hsl
                for j in range(4):
                    b = bank()
                    S.pe(MM(P[:, b, 0:n], [(w1g[:, kc, j * 128:(j + 1) * 128], hb[:, kc, o:o + n]) for kc in range(KC)]),
                         r=[("W", s1), ("h", t)], w=[PSR(b)])
                    rs_ = cnt["r"] % 3
                    cnt["r"] += 1
                    S.act(L("activation", out=rl[rs_][:, 0:n], in_=P[:, b, 0:n], func=AF.Relu), r=[PSR(b)], w=[pfx + "rl%d" % rs_])
                    S.act(L("activation", out=hid[hsl][:, j, 0:n], in_=rl[rs_][:, 0:n], func=AF.Square),
                          r=[pfx + "rl%d" % rs_], w=[hres])
                return hsl

            def stageB(g, t, hsl):
                w1g, w2g, s1, s2 = wts[g]
                o, n = TILES[t]
                col = colof(s, t)
                hres = pfx + "hid%d" % hsl
                for c in range(KC):
                    b = bank()
                    S.pe(MM(P[:, b, 0:n], [(w2g[:, j, c * 128:(c + 1) * 128], hid[hsl][:, j, 0:n]) for j in range(4)]),
                         r=[("W", s2), hres], w=[PSR(b)])
                    S.dve(L("scalar_tensor_tensor", out=xT[:, c, o:o + n], in0=P[:, b, 0:n], scalar=G2(l, c, col), in1=xT[:, c, o:o + n],
                            op0=ALU.mult, op1=ALU.add), r=[PSR(b), "mod%d" % l, ("x", t)], w=[("x", t)])

            pend = []
            for g in range(NG):
                for ti, t in enumerate(tiles):
                    hsl = stageA(g, t)
                    pend.append((g, t, hsl))
                    if len(pend) > 1:
                        stageB(*pend.pop(0))
                    if ti == 0 and g + 1 < NG:
                        wts[g + 1] = mlp_prefetch(l, g + 1)
                    if deferred_mod["gen"] is not None:
                        try:
                            next(deferred_mod["gen"])
                        except StopIteration:
                            deferred_mod["gen"] = None
            while pend:
                stageB(*pend.pop(0))
            if deferred_mod["gen"] is not None:
                for _ in deferred_mod["gen"]:
                    pass
                deferred_mod["gen"] = None

        def final_phase(s):
            pfx = phase("fin", fence=False)
            sq = ralloc([KC, 512], BF16)
            sd = ralloc([512], F32)
            rstd = ralloc([512], F32)
            tmp = ralloc([KC, 512], F32)
            xo = [ralloc([1024], F32) for _ in range(2)]
            oc = 0
            for t in range(4):
                o, n = TILES[t]
                rms_stats(pfx, t, sq, sd, rstd)
                S.dve(L("tensor_tensor", out=tmp[:, :, 0:n], in0=xT[:, :, o:o + n],
                        in1=rstd[:, 0:n].unsqueeze(1).to_broadcast([128, KC, n]), op=ALU.mult),
                      r=[("x", t), pfx + "rstd"], w=[pfx + "tmp"])
                for kc in range(KC):
                    S.act(L("activation", out=tmp[:, kc, 0:n], in_=tmp[:, kc, 0:n], func=AF.Copy, scale=cpk[:, CP_FG + kc:CP_FG + kc + 1]),
                          r=[pfx + "tmp", "cpk"], w=[pfx + "tmp"])
                for j in range(4):
                    xs = oc % 2
                    oc += 1
                    for half in range(2):
                        b = bank()

                        def fn(e, b=b, half=half, j=j):
                            ins = None
                            for q in range(4):
                                kc = half * 4 + q
                                ins = e.transpose(P[:, b, q * 128:(q + 1) * 128], tmp[:, kc, j * 128:(j + 1) * 128], identf)
                            return ins
                        S.pe(fn, r=[pfx + "tmp", "cpk"], w=[PSR(b)])
                        if half == 0:
                            S.dve(L("tensor_copy", out=xo[xs][:, 0:512], in_=P[:, b, :]), r=[PSR(b)], w=[pfx + "xo%d" % xs])
                        else:
                            S.act(L("activation", out=xo[xs][:, 512:1024], in_=P[:, b, :], func=AF.Copy), r=[PSR(b)], w=[pfx + "xo%d" % xs])
                    S.dma_sp(L("dma_start", out=out_d[s, o + j * 128:o + (j + 1) * 128, :], in_=xo[xs]), r=[pfx + "xo%d" % xs])

        for s in range(nseq):
            load_x(s)
            if debug and s == 0:
                dump("x0", xT[:], [("x", t) for t in range(5)])
            for l in range(nlayers):
                last = (l == DEPTH - 1)
                all5 = list(range(5))
                otiles = [0, 1, 2, 3] if last else all5
                norm_phase(s, l, 1, all5, fence=not (l > 0 and s > 0))
                if debug and s == 0 and l == 0:
                    dump("h", hb[:], [("h", t) for t in range(5)])
                if stop_after == "norm1":
                    break
                conv_phase(s, l, otiles)
                if stop_after == "conv":
                    dump_y()
                    break
                lru_phase(s, l, otiles)
                if stop_after == "lru":
                    dump_y()
                    break
                attn_phase(s, l, do_ctx_queries=not last)
                if debug and s == 0 and l == 0:
                    dump_y()
                if stop_after == "mix":
                    break
                pre = mlp_prefetch(l, 0)
                wout_phase(s, l, otiles)
                if debug and s == 0 and l == 0:
                    dump("x1", xT[:], [("x", t) for t in range(5)])
                norm_phase(s, l, 2, otiles)
                if s == 0 and l == 0 and nlayers > 1:
                    deferred_mod["gen"] = "start"
                mlp_phase(s, l, otiles, pre)
                if debug and s == 0 and l == 0:
                    dump("x2", xT[:], [("x", t) for t in range(5)])
            final_phase(s)
        S.run(st)
    return nc, dbg


def fm(v, n):
    return np.ascontiguousarray(np.asarray(v, np.float32).reshape(n, 128).T)


def make_in_maps(inp):
    g = {k: np.asarray(v) for k, v in inp.items()}
    small = np.zeros((DEPTH, 128, NS), np.float32)
    bdw = np.zeros((DEPTH, 128, 8, 128), np.float32)
    bb = np.zeros((DEPTH, 8, 128, NB, 128), np.float32)
    dgw = np.zeros((DEPTH, 128, 62, 128), np.float32)
    dglw = np.zeros((DEPTH, 128, 16, 128), np.float32)
    pidx = np.arange(128)
    for l in range(DEPTH):
        sm = small[l]
        sm[:, SM["n1g"]:SM["n1g"] + 8] = fm(g["norm1_g"][l], 8)
        sm[:, SM["n2g"]:SM["n2g"] + 8] = fm(g["norm2_g"][l], 8)
        cw = g["conv_w"][l]
        for c in range(2):
            sm[:, SM["cw"] + c * 31:SM["cw"] + (c + 1) * 31] = cw[:, c * 128:(c + 1) * 128].T
        sm[:, SM["cb"]:SM["cb"] + 2] = fm(g["conv_b"][l], 2)
        sm[:, SM["lng"]:SM["lng"] + 2] = fm(g["conv_ln_g"][l], 2)
        sm[:, SM["lnb"]:SM["lnb"] + 2] = fm(g["conv_ln_b"][l], 2)
        for d in range(2):
            for c in range(2):
                pi = d * 2 + c
                sm[:, SM["lcw"] + pi * 4:SM["lcw"] + pi * 4 + 4] = g["lru_conv_w"][l, d][:, c * 128:(c + 1) * 128].T
                sm[:, SM["lcb"] + pi] = g["lru_conv_b"][l, d][c * 128:(c + 1) * 128]
                sm[:, SM["lbx"] + pi] = g["lru_bx"][l, d][c * 128:(c + 1) * 128]
                sm[:, SM["lba"] + pi] = g["lru_ba"][l, d][c * 128:(c + 1) * 128]
                sm[:, SM["lam"] + pi] = g["lru_lambda"][l, d][c * 128:(c + 1) * 128]
                for xa, nm in enumerate(("lru_wx", "lru_wa")):
                    wblk = g[nm][l, d]
                    i = (d * 2 + xa) * 2 + c
                    bdw[l, 0:64, i, 0:64] = wblk[2 * c]
                    bdw[l, 64:128, i, 64:128] = wblk[2 * c + 1]
        sm[:, SM["adab"]:SM["adab"] + 48] = fm(g["ada_b"][l], 48)
        dgw[l, pidx, :, pidx] = sm[:, SM["cw"]:SM["cw"] + 62]
        dglw[l, pidx, :, pidx] = sm[:, SM["lcw"]:SM["lcw"] + 16]
        bb[l] = build_bias_blocks(g["na_rpb"][l])
    maps = []
    for i in range(NCORES):
        cpack = np.zeros((128, NCP), np.float32)
        cpack[:, CP_ID:CP_ID + 128] = np.eye(128, dtype=np.float32)
        cpack[:, CP_FG:CP_FG + 8] = fm(g["final_g"], 8)
        cvecs = [g["c"][2 * i], g["c"][2 * i + 1], g["c_ctx"]]
        ct = np.stack([fm(v, 8) for v in cvecs], axis=-1)
        cpack[:, CP_CT:CP_CT + 24] = ct.reshape(128, 24)
        maps.append({
            "x": np.ascontiguousarray(g["x"][2 * i:2 * i + 2]),
            "ctx": np.ascontiguousarray(g["ctx"][2 * i:2 * i + 2]),
            "cpack": cpack, "small": small, "ada_w": g["ada_w"], "w_in": g["w_in"], "w_out": g["w_out"],
            "w1": g["mlp_w1"], "w2": g["mlp_w2"], "bdw": bdw, "bb": bb, "dgw": dgw, "dglw": dglw,
        })
    return maps


_NC_CACHE = {}


def kernel(**inputs):
    if "nc" not in _NC_CACHE:
        _NC_CACHE["nc"] = build_program()[0]
    nc = _NC_CACHE["nc"]
    maps = make_in_maps(inputs)
    res = run_bass_kernel_spmd(nc, maps, core_ids=list(range(NCORES)))
    out = np.concatenate([np.asarray(r["out"]) for r in res.results], axis=0)
    return out.astype(np.float32)
```

```python
import contextlib
import numpy as np
import concourse.bass as bass
import concourse.mybir as mybir
from concourse.bass_utils import run_bass_kernel_spmd

F32 = mybir.dt.float32
BF16 = mybir.dt.bfloat16
AF = mybir.ActivationFunctionType
ALU = mybir.AluOpType

NCORES = 8
DEPTH = 2
D = 1024
KC = 8
TL = 2048
TC = 256
T = TL + TC
TILES = [(0, 512), (512, 512), (1024, 512), (1536, 512), (2048, 256)]
EPS = 1e-6
NEG = -200.0

SM = {}
_o = 0
for _n, _w in [("n1g", 8), ("n2g", 8), ("cw", 62), ("cb", 2), ("lng", 2), ("lnb", 2), ("lcw", 16),
               ("lcb", 4), ("lbx", 4), ("lba", 4), ("lam", 4), ("adab", 48)]:
    SM[_n] = _o
    _o += _w
NS = _o
CP_ID, CP_FG, CP_CT, NCP = 0, 128, 136, 160


class Op:
    __slots__ = ("eng", "fn", "deps", "dma", "slot", "semval", "tick", "signal", "prev_slot_op")


class Sched:
    ENGS = ["sp", "act", "dve", "pool", "pe"]
    NSLOT = 8

    def __init__(self, nc):
        self.nc = nc
        self.q = {e: [] for e in self.ENGS}
        self.lastw = {}
        self.readers = {}
        self.ndma = {e: 0 for e in self.ENGS}
        self.slot_last = {}
        self.sem = {}
        self.cur_fence = []
        self.fenced = set()

    def fence(self):
        ops = []
        for e in self.ENGS:
            if self.q[e]:
                ops.append(self.q[e][-1])
        for o in self.slot_last.values():
            ops.append(o)
        self.cur_fence = ops
        self.fenced = set()

    def add(self, eng, fn, r=(), w=(), dma=False):
        o = Op()
        o.eng = eng
        o.fn = fn
        o.dma = dma
        o.signal = False
        o.tick = None
        o.slot = None
        o.semval = None
        o.prev_slot_op = None
        deps = []
        for x in r:
            lw = self.lastw.get(x)
            if lw is not None:
                deps.append((lw, "raw"))
        for x in w:
            lw = self.lastw.get(x)
            if lw is not None:
                deps.append((lw, "waw"))
            for rd in self.readers.get(x, ()):
                deps.append((rd, "war"))
            if isinstance(x, str) and x.startswith("R:") and x not in self.fenced:
                self.fenced.add(x)
                for fo in self.cur_fence:
                    deps.append((fo, "raw"))
        o.deps = deps
        for x in r:
            self.readers.setdefault(x, []).append(o)
        for x in w:
            self.lastw[x] = o
            self.readers[x] = []
        if dma:
            n = self.ndma[eng]
            self.ndma[eng] = n + 1
            o.slot = n % self.NSLOT
            o.semval = 16 * (n // self.NSLOT + 1)
            key = (eng, o.slot)
            o.prev_slot_op = self.slot_last.get(key)
            self.slot_last[key] = o
        self.q[eng].append(o)
        return o

    def pe(self, fn, r=(), w=()):
        return self.add("pe", fn, r, w)

    def act(self, fn, r=(), w=()):
        return self.add("act", fn, r, w)

    def dve(self, fn, r=(), w=()):
        return self.add("dve", fn, r, w)

    def pool(self, fn, r=(), w=()):
        return self.add("pool", fn, r, w)

    def dma_sp(self, fn, r=(), w=()):
        return self.add("sp", fn, r, w, dma=True)

    def dma_pool(self, fn, r=(), w=()):
        return self.add("pool", fn, r, w, dma=True)

    @staticmethod
    def _skip(d, o):
        return (not d.dma) and (not o.dma) and d.eng == o.eng and o.eng == "pe"

    @staticmethod
    def _skip_kind(d, o, kind):
        return (not d.dma) and (not o.dma) and d.eng == o.eng and (o.eng == "pe" or kind in ("war", "waw"))

    def finalize(self):
        for e in self.ENGS:
            for o in self.q[e]:
                for (d, kind) in o.deps:
                    if d is o or d.dma:
                        continue
                    if self._skip_kind(d, o, kind):
                        continue
                    d.signal = True
        for e in self.ENGS:
            c = 0
            for o in self.q[e]:
                if o.signal and not o.dma:
                    c += 1
                    o.tick = c

    def alloc_sems(self, stack):
        nc = self.nc
        for e in self.ENGS:
            self.sem[("eng", e)] = stack.enter_context(nc.semaphore("s_" + e))
        for e in ("sp", "pool"):
            for s in range(self.NSLOT):
                self.sem[("dma", e, s)] = stack.enter_context(nc.semaphore("d_%s%d" % (e, s)))

    def emit(self, e, eng):
        seen = {}
        for o in self.q[e]:
            waits = {}
            for (d, kind) in o.deps:
                if d is o:
                    continue
                if d.dma:
                    key = ("dma", d.eng, d.slot)
                    val = d.semval
                else:
                    if self._skip_kind(d, o, kind):
                        continue
                    key = ("eng", d.eng)
                    val = d.tick
                if waits.get(key, 0) < val:
                    waits[key] = val
            if o.dma and o.prev_slot_op is not None:
                key = ("dma", e, o.slot)
                val = o.prev_slot_op.semval
                if waits.get(key, 0) < val:
                    waits[key] = val
            for key, val in waits.items():
                if seen.get(key, 0) >= val:
                    continue
                eng.wait_ge(self.sem[key], val)
                seen[key] = val
            ins = o.fn(eng)
            if o.dma:
                ins.then_inc(self.sem[("dma", e, o.slot)], 16)
            elif o.signal:
                ins.then_inc(self.sem[("eng", e)], 1)
        if e in ("sp", "pool"):
            for s in range(self.NSLOT):
                lo = self.slot_last.get((e, s))
                if lo is not None and seen.get(("dma", e, s), 0) < lo.semval:
                    eng.wait_ge(self.sem[("dma", e, s)], lo.semval)

    def run(self, stack):
        nc = self.nc
        self.finalize()
        block = stack.enter_context(nc.Block())
        S = self

        @block.sync
        def _(eng):
            S.emit("sp", eng)

        @block.scalar
        def _(eng):
            S.emit("act", eng)

        @block.vector
        def _(eng):
            S.emit("dve", eng)

        @block.gpsimd
        def _(eng):
            S.emit("pool", eng)

        @block.tensor
        def _(eng):
            S.emit("pe", eng)


def L(f, *a, **k):
    return lambda e: getattr(e, f)(*a, **k)


def MM(out, pairs, first=True, last=True):
    def fn(e):
        n = len(pairs)
        ins = None
        for i, (l, rh) in enumerate(pairs):
            ins = e.matmul(out, lhsT=l, rhs=rh, start=(first and i == 0), stop=(last and i == n - 1))
        return ins
    return fn


def _rs(r):
    return min(max(r - 4, 0), 24)


def _valid(r, k):
    return _rs(r) <= k < _rs(r) + 8


def attn_tables():
    info = {}
    runs = []
    interior_base = None
    for qt in range(16):
        r0 = 2 * qt
        kts = [kt for kt in range(16)
               if any(_valid(r0 + a, 2 * kt + b) for a in (0, 1) for b in (0, 1))]
        interior = (4 <= r0 <= 26)
        if interior:
            if interior_base is None:
                interior_base = sum(len(k) for _, k in runs)
                runs.append((r0, kts))
            info[qt] = (kts, interior_base)
        else:
            base = sum(len(k) for _, k in runs)
            runs.append((r0, kts))
            info[qt] = (kts, base)
    nb = sum(len(k) for _, k in runs)
    return info, runs, nb


ATT_INFO, ATT_RUNS, NB = attn_tables()


def build_bias_blocks(rpb_l):
    H = rpb_l.shape[0]
    out = np.full((H, 128, NB, 128), NEG, np.float32)
    p = np.arange(128)
    kp, kcol = p // 64, p % 64
    qp, qcol = p // 64, p % 64
    cstart = np.clip(qcol - 8, 0, 48)
    colvalid = (kcol[:, None] >= cstart[None, :]) & (kcol[:, None] < cstart[None, :] + 16)
    cidx = np.clip(kcol[:, None] - qcol[None, :], -15, 15) + 15
    n = 0
    for (r0, kts) in ATT_RUNS:
        for kt in kts:
            k0 = 2 * kt
            kr = k0 + kp[:, None]
            qr = r0 + qp[None, :]
            delta = kr - qr
            rowvalid = np.zeros((128, 128), bool)
            for a in (0, 1):
                for b in (0, 1):
                    if _valid(r0 + a, k0 + b):
                        rowvalid |= (qp[None, :] == a) & (kp[:, None] == b)
            ok = rowvalid & colvalid & (np.abs(delta) <= 7)
            ridx = np.clip(delta + 7, 0, 14)
            vals = rpb_l[:, ridx, cidx]
            out[:, :, n, :] = np.where(ok[None], vals, NEG)
            n += 1
    return out


def build_program(debug=False, nseq=2, nlayers=DEPTH, stop_after=None):
    nc = bass.Bass("TRN2", target_bir_lowering=False)
    x_d = nc.dram_tensor("x", [2, TL, D], F32, kind="ExternalInput").ap()
    ctx_d = nc.dram_tensor("ctx", [2, TC, D], F32, kind="ExternalInput").ap()
    cp_d = nc.dram_tensor("cpack", [128, NCP], F32, kind="ExternalInput").ap()
    sm_d = nc.dram_tensor("small", [DEPTH, 128, NS], F32, kind="ExternalInput").ap()
    adaw_d = nc.dram_tensor("ada_w", [DEPTH, D, 6 * D], F32, kind="ExternalInput").ap()
    win_d = nc.dram_tensor("w_in", [DEPTH, D, 2560], F32, kind="ExternalInput").ap()
    wout_d = nc.dram_tensor("w_out", [DEPTH, D, D], F32, kind="ExternalInput").ap()
    w1_d = nc.dram_tensor("w1", [DEPTH, D, 4 * D], F32, kind="ExternalInput").ap()
    w2_d = nc.dram_tensor("w2", [DEPTH, 4 * D, D], F32, kind="ExternalInput").ap()
    bdw_d = nc.dram_tensor("bdw", [DEPTH, 128, 8, 128], F32, kind="ExternalInput").ap()
    bb_d = nc.dram_tensor("bb", [DEPTH, 8, 128, NB, 128], F32, kind="ExternalInput").ap()
    dgw_d = nc.dram_tensor("dgw", [DEPTH, 128, 62, 128], F32, kind="ExternalInput").ap()
    dglw_d = nc.dram_tensor("dglw", [DEPTH, 128, 16, 128], F32, kind="ExternalInput").ap()
    out_d = nc.dram_tensor("out", [2, TL, D], F32, kind="ExternalOutput").ap()
    yscr = nc.dram_tensor("yscr", [128, 8, T], BF16, kind="Internal").ap()
    dbg = {}

    st = contextlib.ExitStack()
    with st:
        def sb(name, shape, dt):
            return st.enter_context(nc.sbuf_tensor(name, shape, dt))

        xT = sb("xT", [128, KC, T], F32)
        hb = sb("hb", [128, KC, T], BF16)
        W = sb("W", [128, 4, 4096], BF16)
        RC = 15872
        R = sb("R", [128, RC], F32)
        cpk = sb("cpk", [128, NCP], F32)
        smk = sb("smk", [128, DEPTH, NS], F32)
        identb = sb("identb", [128, 128], BF16)
        onesm = sb("onesm", [128, 128], BF16)
        ones256 = sb("ones256", [128, 128], BF16)
        zerosb = sb("zerosb", [128, 256], BF16)
        sct = sb("sct", [128, KC, 3], F32)
        modT = sb("modT", [128, DEPTH, 48, 3], F32)
        A1 = sb("A1", [128, DEPTH, 8, 3], F32)
        A2 = sb("A2", [128, DEPTH, 8, 3], F32)
        clt = sb("clt", [128, DEPTH, 4], F32)
        clt2 = sb("clt2", [128, DEPTH, 4], F32)
        epst = sb("epst", [128, 1], F32)
        hexp = sb("hexp", [128, 2], F32)
        lruc = sb("lruc", [128, DEPTH, 16], F32)
        P = st.enter_context(nc.psum_tensor("P", [128, 7, 512], F32))
        Pb = st.enter_context(nc.psum_tensor("Pb", [128, 1024], BF16))

        S = Sched(nc)
        S.alloc_sems(st)
        identf = cpk[:, CP_ID:CP_ID + 128]

        bank_i = [0]
        pair_i = [0]

        def bank():
            b = bank_i[0] % 7
            bank_i[0] += 1
            return b

        def bank2():
            b = 2 * (pair_i[0] % 3)
            pair_i[0] += 1
            return b

        def PSR(b):
            return ("ps", b)

        phase_n = [0]
        ar_off = [0]

        ar_top = [False, RC]

        TOP_WO = 4096
        TOP_MLP = 3584
        LOW_LIMIT = RC - TOP_WO - TOP_MLP

        def phase(tag, fence=True, top=False, top_base=None):
            phase_n[0] += 1
            ar_off[0] = 0
            ar_top[0] = top
            ar_top[1] = RC if top_base is None else top_base
            if fence:
                S.fence()
                return "R:%d%s:" % (phase_n[0], tag)
            return "U:%d%s:" % (phase_n[0], tag)

        def ralloc(shape, dt):
            n = 1
            for v in shape:
                n *= v
            nb = n * (2 if dt == BF16 else 4)
            ncols = (nb + 3) // 4
            if ar_top[0]:
                ar_top[1] -= ncols
                c0 = ar_top[1]
                assert c0 >= LOW_LIMIT, ("top arena overflow", c0)
            else:
                c0 = ar_off[0]
                ar_off[0] += ncols
                assert ar_off[0] <= RC, ("arena overflow", ar_off[0])
            ap = R[:, c0:c0 + ncols]
            if dt == BF16:
                ap = ap.bitcast(BF16)
                if ncols * 2 != n:
                    ap = ap[:, 0:n]
            if len(shape) == 2:
                return ap.rearrange("p (a b) -> p a b", a=shape[0])
            if len(shape) == 3:
                return ap.rearrange("p (a b c) -> p a b c", a=shape[0], b=shape[1])
            if len(shape) == 4:
                return ap.rearrange("p (a b c d) -> p a b c d", a=shape[0], b=shape[1], c=shape[2])
            return ap

        def dump(name, ap, res):
            if not debug:
                return
            shp = list(ap.shape)
            t = nc.dram_tensor("dbg_" + name, shp, ap.dtype, kind="ExternalOutput").ap()
            dbg[name] = t
            S.dma_sp(L("dma_start", out=t, in_=ap), r=res)

        S.dma_sp(L("dma_start", out=cpk[:], in_=cp_d), w=["cpk"])
        S.dma_sp(L("dma_start", out=smk[:], in_=sm_d.rearrange("l p n -> p l n")), w=["smk"])
        S.dve(L("tensor_copy", out=identb[:], in_=identf), r=["cpk"], w=["const"])
        S.pool(L("memset", onesm[:], 1.0 / 1024.0), w=["const"])
        S.pool(L("memset", ones256[:], 1.0 / 256.0), w=["const"])
        S.pool(L("memset", zerosb[:], 0.0), w=["const"])
        S.pool(L("memset", epst[:], EPS), w=["const"])
        S.pool(L("memset", hexp[:, 0:1], -0.5), w=["const"])
        S.pool(L("memset", hexp[:, 1:2], 0.5), w=["const"])
        S.act(L("activation", out=sct[:], in_=cpk[:, CP_CT:CP_CT + 24].rearrange("p (a b) -> p a b", a=8), func=AF.Silu),
              r=["cpk"], w=["sct"])

        def smc(l, name, i=0, n=1):
            o = SM[name] + i
            return smk[:, l, o:o + n]

        def mod_steps(l, nbuf, cw=512):
            pfx = phase("mod")
            awb = [ralloc([KC, cw], F32) for _ in range(nbuf)]
            modrow = ralloc([6144], F32)
            assert nbuf == 2 or ar_off[0] <= LOW_LIMIT
            mres = "mod%d" % l
            for q in range(6144 // cw):
                slot = q % nbuf
                S.dma_sp(L("dma_start", out=awb[slot], in_=adaw_d[l].rearrange("(a p) n -> p a n", p=128)[:, :, q * cw:(q + 1) * cw]),
                         w=[pfx + "aw%d" % slot])
                b = bank()
                S.pe(MM(P[0:3, b, 0:cw], [(sct[:, kc, :], awb[slot][:, kc, :]) for kc in range(KC)]), r=[pfx + "aw%d" % slot, "sct"], w=[PSR(b)])
                if q % 2 == 0:
                    S.dve(L("tensor_copy", out=modrow[0:3, q * cw:(q + 1) * cw], in_=P[0:3, b, 0:cw]), r=[PSR(b)], w=[pfx + "modrow"])
                else:
                    S.act(L("activation", out=modrow[0:3, q * cw:(q + 1) * cw], in_=P[0:3, b, 0:cw], func=AF.Copy), r=[PSR(b)], w=[pfx + "modrow"])
                yield q
            b = bank()
            mp = P[:, b, 0:144]

            def fnT(e, mp=mp):
                ins = None
                for j in range(48):
                    ins = e.transpose(mp[:, j * 3:(j + 1) * 3], modrow[0:3, j * 128:(j + 1) * 128], identf[0:3, 0:3])
                return ins
            S.pe(fnT, r=[pfx + "modrow", "cpk"], w=[PSR(b)])
            S.dve(L("tensor_tensor", out=modT[:, l, :, :], in0=mp.rearrange("p (a b) -> p a b", b=3),
                    in1=smc(l, "adab", 0, 48).unsqueeze(2).to_broadcast([128, 48, 3]), op=ALU.add),
                  r=[PSR(b), "smk"], w=[mres])
            for (A, gname, j0) in ((A1, "n1g", 8), (A2, "n2g", 32)):
                S.dve(L("scalar_tensor_tensor", out=A[:, l, :, :], in0=modT[:, l, j0:j0 + 8, :], scalar=1.0,
                        in1=smc(l, gname, 0, 8).unsqueeze(2).to_broadcast([128, 8, 3]), op0=ALU.add, op1=ALU.mult),
                      r=[mres, "smk"], w=[mres])
            cres = "clt%d" % l
            S.act(L("activation", out=clt[:, l, :], in_=smc(l, "lam", 0, 4), func=AF.Exp, scale=-1.0), r=["smk"], w=[cres])
            S.act(L("activation", out=clt[:, l, :], in_=clt[:, l, :], func=AF.Ln, bias=1.0), r=[cres], w=[cres])
            S.dve(L("tensor_scalar", out=lruc[:, l, 0:4], in0=clt[:, l, :], scalar1=-8.0, scalar2=None, op0=ALU.mult), r=[cres], w=[cres + "b"])
            S.dve(L("tensor_scalar", out=lruc[:, l, 4:8], in0=clt[:, l, :], scalar1=-4.0, scalar2=None, op0=ALU.mult), r=[cres], w=[cres + "b"])
            S.dve(L("tensor_scalar", out=lruc[:, l, 8:12], in0=smc(l, "lbx", 0, 4), scalar1=0.5, scalar2=None, op0=ALU.mult), r=["smk"], w=[cres + "b"])
            S.dve(L("tensor_scalar", out=lruc[:, l, 12:16], in0=smc(l, "lba", 0, 4), scalar1=0.5, scalar2=None, op0=ALU.mult), r=["smk"], w=[cres + "b"])
            yield 99

        for _ in mod_steps(0, 2):
            pass
        deferred_mod = {"gen": None}
        if debug:
            dump("modT", modT[:], ["mod0"])

        def dump_y():
            if not debug:
                return
            pfx = phase("dbg")
            t_ = nc.dram_tensor("dbg_y", [128, 8, T], BF16, kind="ExternalOutput").ap()
            dbg["y"] = t_
            for t in range(5):
                o, n = TILES[t]
                buf = ralloc([8, 512], BF16)
                S.dma_sp(L("dma_start", out=buf[:, :, 0:n], in_=yscr[:, :, o:o + n]), r=[("yscr", t)], w=[pfx + "b%d" % t])
                S.dma_sp(L("dma_start", out=t_[:, :, o:o + n], in_=buf[:, :, 0:n]), r=[pfx + "b%d" % t])

        def SH1(l, kc, col): return modT[:, l, 0 + kc, col:col + 1]
        def G1(l, kc, col): return modT[:, l, 16 + kc, col:col + 1]
        def SH2(l, kc, col): return modT[:, l, 24 + kc, col:col + 1]
        def G2(l, kc, col): return modT[:, l, 40 + kc, col:col + 1]

        def colof(s, t):
            return 2 if t == 4 else s

        def load_x(s):
            pfx = phase("ld")
            xin = [ralloc([1024], F32) for _ in range(2)]
            for i in range(18):
                src = x_d[s, i * 128:(i + 1) * 128, :] if i < 16 else ctx_d[s, (i - 16) * 128:(i - 15) * 128, :]
                slot = i % 2
                t = min(i // 4, 4)
                S.dma_sp(L("dma_start", out=xin[slot], in_=src), w=[pfx + "xin%d" % slot])
                for half in range(2):
                    b = bank()

                    def fn(e, b=b, half=half, slot=slot):
                        ins = None
                        for q in range(4):
                            kc = half * 4 + q
                            ins = e.transpose(P[:, b, q * 128:(q + 1) * 128], xin[slot][:, kc * 128:(kc + 1) * 128], identf)
                        return ins
                    S.pe(fn, r=[pfx + "xin%d" % slot, "cpk"], w=[PSR(b)])
                    dst = xT[:, half * 4:half * 4 + 4, i * 128:(i + 1) * 128]
                    src_ps = P[:, b, :].rearrange("p (a n) -> p a n", a=4)
                    if half == 0:
                        S.dve(L("tensor_copy", out=dst, in_=src_ps), r=[PSR(b)], w=[("x", t)])
                    else:
                        S.act(L("activation", out=dst, in_=src_ps, func=AF.Copy), r=[PSR(b)], w=[("x", t)])

        def rms_stats(pfx, t, sq, sd, rstd, tag=""):
            o, n = TILES[t]
            S.act(L("activation", out=sq[:, :, 0:n], in_=xT[:, :, o:o + n], func=AF.Square), r=[("x", t)], w=[pfx + "sq" + tag])
            b = bank()
            S.pe(MM(P[:, b, 0:n], [(onesm[:], sq[:, kc, 0:n]) for kc in range(KC)]), r=[pfx + "sq" + tag, "const"], w=[PSR(b)])
            S.act(L("activation", out=sd[:, 0:n], in_=P[:, b, 0:n], func=AF.Ln, bias=epst[:, 0:1]), r=[PSR(b), "const"], w=[pfx + "sd" + tag])
            S.act(L("activation", out=rstd[:, 0:n], in_=sd[:, 0:n], func=AF.Exp, scale=-0.5), r=[pfx + "sd" + tag], w=[pfx + "rstd" + tag])

        def norm_phase(s, l, which, tiles, fence=True):
            pfx = phase("nrm", fence=fence)
            _sq = ralloc([KC, 512], BF16)
            _sd = ralloc([512], F32)
            _rstd = ralloc([512], F32)
            sq = [_sq, _sq]
            sd = [_sd, _sd]
            rstd = [_rstd, _rstd]
            tmp = [ralloc([4, 512], F32) for _ in range(2)]
            assert ar_off[0] <= LOW_LIMIT
            A = A1 if which == 1 else A2
            SH = SH1 if which == 1 else SH2
            hc_ = 0
            for ti, t in enumerate(tiles):
                o, n = TILES[t]
                col = colof(s, t)
                k = 0
                tg = "0"
                rms_stats(pfx, t, sq[k], sd[k], rstd[k], tg)
                for half in range(2):
                    k2 = hc_ % 2
                    hc_ += 1
                    tg2 = "%d" % k2
                    S.dve(L("tensor_tensor", out=tmp[k2][:, :, 0:n], in0=xT[:, half * 4:half * 4 + 4, o:o + n],
                            in1=rstd[k][:, 0:n].unsqueeze(1).to_broadcast([128, 4, n]), op=ALU.mult),
                          r=[("x", t), pfx + "rstd" + tg], w=[pfx + "tmp" + tg2])
                    for q in range(4):
                        kc = half * 4 + q
                        if kc % 2 == 0:
                            S.act(L("activation", out=hb[:, kc, o:o + n], in_=tmp[k2][:, q, 0:n], func=AF.Identity,
                                    scale=A[:, l, kc, col:col + 1], bias=SH(l, kc, col)),
                                  r=[pfx + "tmp" + tg2, "mod%d" % l], w=[("h", t)])
                        else:
                            S.dve(L("tensor_scalar", out=hb[:, kc, o:o + n], in0=tmp[k2][:, q, 0:n],
                                    scalar1=A[:, l, kc, col:col + 1], scalar2=SH(l, kc, col), op0=ALU.mult, op1=ALU.add),
                                  r=[pfx + "tmp" + tg2, "mod%d" % l], w=[("h", t)])

        def load_w512(l, slot, c0):
            dst = W[:, slot, :].rearrange("p (a b) -> p a b", a=KC)
            S.dma_pool(L("dma_start", out=dst, in_=win_d[l].rearrange("(a p) n -> p a n", p=128)[:, :, c0:c0 + 512]),
                       w=[("W", slot)])
            return dst

        ystage_i = [0]

        def conv_phase(s, l, tiles):
            pfx = phase("cv")
            wA = load_w512(l, 0, 0)
            ub = ralloc([2, 2364], BF16)
            dg = ralloc([2, 31, 128], BF16)
            sig = [ralloc([512], F32) for _ in range(2)]
            cv = [ralloc([2, 512], F32) for _ in range(2)]
            cvb = ralloc([2, 512], BF16)
            sqc = ralloc([2, 512], BF16)
            mean = ralloc([512], F32)
            m2 = ralloc([512], F32)
            sd = ralloc([512], F32)
            rstd = ralloc([512], F32)
            dd = ralloc([2, 512], F32)
            ys = [ralloc([2, 512], BF16) for _ in range(2)]
            S.dma_pool(L("dma_start", out=dg.rearrange("p a b c -> p (a b) c"), in_=dgw_d[l]), w=[pfx + "dg"])
            for (a0, a1) in ((0, 15), (2063, 2093), (2349, 2364)):
                S.pool(L("memset", ub[:, :, a0:a1], 0.0), w=[pfx + "ub"])

            def upos(t):
                o, n = TILES[t]
                return (15 + o) if t < 4 else (2078 + 15)

            for t in tiles:
                o, n = TILES[t]
                bs = [bank() for _ in range(4)]
                for j in range(4):
                    S.pe(MM(P[:, bs[j], 0:n], [(wA[:, kc, j * 128:(j + 1) * 128], hb[:, kc, o:o + n]) for kc in range(KC)]),
                         r=[("W", 0), ("h", t)], w=[PSR(bs[j])])
                for c in range(2):
                    S.act(L("activation", out=sig[c][:, 0:n], in_=P[:, bs[2 + c], 0:n], func=AF.Tanh, scale=0.5),
                          r=[PSR(bs[2 + c])], w=[pfx + "sig%d" % c])
                    S.dve(L("scalar_tensor_tensor", out=ub[:, c, upos(t):upos(t) + n], in0=sig[c][:, 0:n], scalar=1.0, in1=P[:, bs[c], 0:n],
                            op0=ALU.add, op1=ALU.mult),
                          r=[PSR(bs[c]), pfx + "sig%d" % c], w=[pfx + "ub"])
            for ti, t in enumerate(tiles):
                o, n = TILES[t]
                base = o if t < 4 else 2078
                bs = [bank() for _ in range(2)]
                k2 = ti % 2
                cvk = cv[k2]
                cvr = pfx + "cv%d" % k2
                for c in range(2):
                    S.pe(MM(P[:, bs[c], 0:n], [(dg[:, c, k, :], ub[:, c, base + k:base + k + n]) for k in range(31)]),
                         r=[pfx + "dg", pfx + "ub"], w=[PSR(bs[c])])
                    S.dve(L("tensor_scalar", out=cvk[:, c, 0:n], in0=P[:, bs[c], 0:n], scalar1=0.5, scalar2=smc(l, "cb", c, 1), op0=ALU.mult, op1=ALU.add),
                          r=[PSR(bs[c]), "smk"], w=[cvr])
                S.act(L("activation", out=cvb[:, :, 0:n], in_=cvk[:, :, 0:n], func=AF.Copy), r=[cvr], w=[pfx + "cvb"])
                S.act(L("activation", out=sqc[:, :, 0:n], in_=cvk[:, :, 0:n], func=AF.Square), r=[cvr], w=[pfx + "sqc"])
                b1 = bank()
                b2 = bank()
                S.pe(MM(P[:, b1, 0:n], [(ones256[:], cvb[:, c, 0:n]) for c in range(2)]), r=["const", pfx + "cvb"], w=[PSR(b1)])
                S.pe(MM(P[:, b2, 0:n], [(ones256[:], sqc[:, c, 0:n]) for c in range(2)]), r=["const", pfx + "sqc"], w=[PSR(b2)])
                S.act(L("activation", out=mean[:, 0:n], in_=P[:, b1, 0:n], func=AF.Copy), r=[PSR(b1)], w=[pfx + "mean"])
                S.act(L("activation", out=m2[:, 0:n], in_=mean[:, 0:n], func=AF.Square), r=[pfx + "mean"], w=[pfx + "m2"])
                S.act(L("activation", out=sd[:, 0:n], in_=P[:, b2, 0:n], func=AF.Identity, bias=epst[:, 0:1]), r=[PSR(b2), "const"], w=[pfx + "sd"])
                S.dve(L("tensor_tensor", out=sd[:, 0:n], in0=sd[:, 0:n], in1=m2[:, 0:n], op=ALU.subtract),
                      r=[pfx + "sd", pfx + "m2"], w=[pfx + "sd"])
                S.act(L("activation", out=sd[:, 0:n], in_=sd[:, 0:n], func=AF.Ln), r=[pfx + "sd"], w=[pfx + "sd"])
                S.act(L("activation", out=rstd[:, 0:n], in_=sd[:, 0:n], func=AF.Exp, scale=-0.5), r=[pfx + "sd"], w=[pfx + "rstd"])
                S.pool(L("tensor_tensor", out=dd[:, :, 0:n], in0=cvk[:, :, 0:n],
                         in1=mean[:, 0:n].unsqueeze(1).to_broadcast([128, 2, n]), op=ALU.subtract),
                       r=[cvr, pfx + "mean"], w=[pfx + "dd"])
                S.dve(L("tensor_tensor", out=dd[:, :, 0:n], in0=dd[:, :, 0:n],
                        in1=rstd[:, 0:n].unsqueeze(1).to_broadcast([128, 2, n]), op=ALU.mult),
                      r=[pfx + "dd", pfx + "rstd"], w=[pfx + "dd"])
                ysl = ti % 2
                for c in range(2):
                    S.act(L("activation", out=ys[ysl][:, c, 0:n], in_=dd[:, c, 0:n], func=AF.Silu,
                            scale=smc(l, "lng", c, 1), bias=smc(l, "lnb", c, 1)),
                          r=[pfx + "dd", "smk"], w=[pfx + "ys%d" % ysl])
                S.dma_sp(L("dma_start", out=yscr[:, 0:2, o:o + n], in_=ys[ysl][:, :, 0:n]), r=[pfx + "ys%d" % ysl], w=[("yscr", t)])

        def lru_phase(s, l, out_tiles):
            pfx = phase("lru")
            wC = load_w512(l, 1, 2048)
            bd = ralloc([8, 128], BF16)
            dgl = ralloc([16, 128], BF16)
            rxb = ralloc([2316], BF16)
            gg = ralloc([T], BF16)
            hsum = ralloc([T], F32)
            tset = []
            for d in range(2):
                tset.append(dict(ucf=ralloc([512], F32), ucb=ralloc([512], BF16), gx=ralloc([512], F32), ga=ralloc([512], F32),
                                 a2=ralloc([512], F32), bt=ralloc([512], F32), hs=[ralloc([512], F32) for _ in range(2)]))
            t1 = ralloc([512], F32)
            ys = [ralloc([512], BF16) for _ in range(2)]
            S.dma_pool(L("dma_start", out=bd, in_=bdw_d[l]), w=[pfx + "bd"])
            S.dma_pool(L("dma_start", out=dgl, in_=dglw_d[l]), w=[pfx + "dgl"])

            def rpos(t):
                o, n = TILES[t]
                return (3 + o) if t < 4 else (2054 + 3)

            order = ([4, 0, 1, 2, 3], [4, 3, 2, 1, 0])
            step_of = [{t: i for i, t in enumerate(order[d])} for d in range(2)]
            ysi = [0]
            for c in range(2):
                cres = "c%d" % c
                for (a0, a1) in ((0, 3), (2051, 2057), (2313, 2316)):
                    S.pool(L("memset", rxb[:, a0:a1], 0.0), w=[pfx + "rxb"])
                for t in range(5):
                    o, n = TILES[t]
                    b = bank()
                    S.pe(MM(P[:, b, 0:n], [(wC[:, kc, c * 128:(c + 1) * 128], hb[:, kc, o:o + n]) for kc in range(KC)]),
                         r=[("W", 1), ("h", t)], w=[PSR(b)])
                    S.dve(L("tensor_copy", out=rxb[:, rpos(t):rpos(t) + n], in_=P[:, b, 0:n]), r=[PSR(b)], w=[pfx + "rxb"])
                    if t in out_tiles:
                        b = bank()
                        S.pe(MM(P[:, b, 0:n], [(wC[:, kc, (2 + c) * 128:(3 + c) * 128], hb[:, kc, o:o + n]) for kc in range(KC)]),
                             r=[("W", 1), ("h", t)], w=[PSR(b)])
                        S.act(L("activation", out=gg[:, o:o + n], in_=P[:, b, 0:n], func=AF.Gelu_apprx_tanh),
                              r=[PSR(b)], w=[pfx + "gg"])
                prev = [None, None]
                prev_res = [None, None]
                for step in range(5):
                    for d in range(2):
                        t = order[d][step]
                        o, n = TILES[t]
                        pi = d * 2 + c
                        ts_ = tset[d]
                        dn = "%d" % d
                        first = (step_of[d][t] < step_of[1 - d][t]) or (step_of[d][t] == step_of[1 - d][t] and d == 0)
                        base = (o if t < 4 else 2054) + (0 if d == 0 else 3)
                        b = bank()
                        S.pe(MM(P[:, b, 0:n], [(dgl[:, pi * 4 + k, :], rxb[:, base + k:base + k + n]) for k in range(4)]),
                             r=[pfx + "dgl", pfx + "rxb"], w=[PSR(b)])
                        S.act(L("activation", out=ts_["ucf"][:, 0:n], in_=P[:, b, 0:n], func=AF.Identity, bias=smc(l, "lcb", pi, 1)),
                              r=[PSR(b), "smk"], w=[pfx + "ucf" + dn])
                        S.act(L("activation", out=ts_["ucb"][:, 0:n], in_=ts_["ucf"][:, 0:n], func=AF.Copy),
                              r=[pfx + "ucf" + dn], w=[pfx + "ucb" + dn])
                        bx_ = bank()
                        ba_ = bank()
                        S.pe(MM(P[:, bx_, 0:n], [(bd[:, (d * 2 + 0) * 2 + c, :], ts_["ucb"][:, 0:n])]), r=[pfx + "bd", pfx + "ucb" + dn], w=[PSR(bx_)])
                        S.pe(MM(P[:, ba_, 0:n], [(bd[:, (d * 2 + 1) * 2 + c, :], ts_["ucb"][:, 0:n])]), r=[pfx + "bd", pfx + "ucb" + dn], w=[PSR(ba_)])
                        lc = lruc[:, l, :]
                        S.act(L("activation", out=ts_["gx"][:, 0:n], in_=P[:, bx_, 0:n], func=AF.Tanh, scale=0.5, bias=lc[:, 8 + pi:9 + pi]),
                              r=[PSR(bx_), "clt%db" % l], w=[pfx + "gx" + dn])
                        S.act(L("activation", out=ts_["ga"][:, 0:n], in_=P[:, ba_, 0:n], func=AF.Tanh, scale=0.5, bias=lc[:, 12 + pi:13 + pi]),
                              r=[PSR(ba_), "clt%db" % l], w=[pfx + "ga" + dn])
                        S.act(L("activation", out=ts_["a2"][:, 0:n], in_=ts_["ga"][:, 0:n], func=AF.Exp, scale=lc[:, pi:pi + 1], bias=lc[:, pi:pi + 1]),
                              r=[pfx + "ga" + dn, "clt%db" % l], w=[pfx + "a2" + dn])
                        S.act(L("activation", out=ts_["ga"][:, 0:n], in_=ts_["ga"][:, 0:n], func=AF.Exp, scale=lc[:, 4 + pi:5 + pi], bias=lc[:, 4 + pi:5 + pi]),
                              r=[pfx + "ga" + dn, "clt%db" % l], w=[pfx + "ga" + dn])
                    for d in range(2):
                        t = order[d][step]
                        o, n = TILES[t]
                        pi = d * 2 + c
                        ts_ = tset[d]
                        dn = "%d" % d
                        lc = lruc[:, l, :]
                        first = (step_of[d][t] < step_of[1 - d][t]) or (step_of[d][t] == step_of[1 - d][t] and d == 0)
                        S.act(L("activation", out=ts_["a2"][:, 0:n], in_=ts_["a2"][:, 0:n], func=AF.Sqrt, scale=-1.0, bias=1.0),
                              r=[pfx + "a2" + dn], w=[pfx + "a2" + dn])
                        S.dve(L("scalar_tensor_tensor", out=ts_["bt"][:, 0:n], in0=ts_["gx"][:, 0:n], scalar=1.0, in1=ts_["ucf"][:, 0:n],
                                op0=ALU.add, op1=ALU.mult),
                              r=[pfx + "gx" + dn, pfx + "ucf" + dn], w=[pfx + "bt" + dn])
                        S.dve(L("scalar_tensor_tensor", out=ts_["bt"][:, 0:n], in0=ts_["bt"][:, 0:n], scalar=0.5, in1=ts_["a2"][:, 0:n],
                                op0=ALU.mult, op1=ALU.mult),
                              r=[pfx + "bt" + dn, pfx + "a2" + dn], w=[pfx + "bt" + dn])
                        need_out = t in out_tiles
                        if first and need_out:
                            dst = hsum[:, o:o + n]
                            dres = pfx + "hsum%d" % t
                        else:
                            hsl = step % 2
                            dst = ts_["hs"][hsl][:, 0:n]
                            dres = pfx + "hs%d_%d" % (d, hsl)
                        init = 0.0 if prev[d] is None else prev[d]
                        rr = [pfx + "ga" + dn, pfx + "bt" + dn] + ([] if prev[d] is None else [prev_res[d]])
                        if d == 0:
                            S.dve(L("tensor_tensor_scan", out=dst, data0=ts_["ga"][:, 0:n], data1=ts_["bt"][:, 0:n], initial=init,
                                    op0=ALU.mult, op1=ALU.add), r=rr, w=[dres])
                            prev[d] = dst[:, n - 1:n]
                        else:
                            S.dve(L("tensor_tensor_scan", out=dst[:, ::-1], data0=ts_["ga"][:, 0:n][:, ::-1], data1=ts_["bt"][:, 0:n][:, ::-1],
                                    initial=init, op0=ALU.mult, op1=ALU.add), r=rr, w=[dres])
                            prev[d] = dst[:, 0:1]
                        prev_res[d] = dres
                        if need_out and not first:
                            S.dve(L("tensor_tensor", out=t1[:, 0:n], in0=dst, in1=hsum[:, o:o + n], op=ALU.add),
                                  r=[dres, pfx + "hsum%d" % t], w=[pfx + "t1"])
                            ysl = ysi[0] % 2
                            ysi[0] += 1
                            S.dve(L("tensor_tensor", out=ys[ysl][:, 0:n], in0=t1[:, 0:n], in1=gg[:, o:o + n], op=ALU.mult),
                                  r=[pfx + "t1", pfx + "gg"], w=[pfx + "ys%d" % ysl])
                            S.dma_sp(L("dma_start", out=yscr[:, 6 + c, o:o + n], in_=ys[ysl][:, 0:n]), r=[pfx + "ys%d" % ysl], w=[("yscr", t)])

        def attn_phase(s, l, do_ctx_queries):
            pfx = phase("att")
            wV = load_w512(l, 2, 1536)
            qk = [ralloc([2, T], BF16) for _ in range(2)]
            vx = ralloc([18, 8, 65], BF16)
            EB = [ralloc([NB, 128], BF16) for _ in range(2)]
            PT = [ralloc([1024], BF16) for _ in range(4)]
            ytok = ralloc([18, 128], BF16)
            yst = [ralloc([512], BF16) for _ in range(2)]
            rden = [ralloc([1], F32) for _ in range(4)]
            S.pool(L("memset", vx[:, :, :, 64:65], 1.0), w=[pfx + "vx1"])
            for i in range(18):
                b = bank()
                S.pe(MM(P[:, b, :], [(hb[:, kc, i * 128:(i + 1) * 128], wV[:, kc, :]) for kc in range(KC)]),
                     r=[("W", 2), ("h", min(i // 4, 4))], w=[PSR(b)])
                src = P[:, b, :].rearrange("p (a n) -> p a n", a=8)
                if i % 2 == 0:
                    S.dve(L("tensor_copy", out=vx[:, i, :, 0:64], in_=src), r=[PSR(b)], w=[pfx + "vx"])
                else:
                    S.act(L("activation", out=vx[:, i, :, 0:64], in_=src, func=AF.Copy), r=[PSR(b)], w=[pfx + "vx"])
            acnt = [0]
            ocnt = [0]
            nqt = 18 if do_ctx_queries else 16
            items = []
            for qt in range(nqt):
                if qt < 16:
                    kts, base = ATT_INFO[qt]
                    items.append((qt, list(kts) + [16, 17], len(kts), base))
                else:
                    items.append((qt, [16, 17], 0, 0))

            def emit_proj(hp):
                sl = hp % 2
                wq = W[:, 3, 0:1024].rearrange("p (a b) -> p a b", a=KC)
                wk = W[:, 3, 1024:2048].rearrange("p (a b) -> p a b", a=KC)
                if hp % 2 == 1:
                    wq = W[:, 3, 2048:3072].rearrange("p (a b) -> p a b", a=KC)
                    wk = W[:, 3, 3072:4096].rearrange("p (a b) -> p a b", a=KC)
                wres = ("W3", hp % 2)
                wsrc = win_d[l].rearrange("(a p) n -> p a n", p=128)
                S.dma_pool(L("dma_start", out=wq, in_=wsrc[:, :, 512 + hp * 128:512 + (hp + 1) * 128]), w=[wres, ("W", 3)])
                S.dma_pool(L("dma_start", out=wk, in_=wsrc[:, :, 1024 + hp * 128:1024 + (hp + 1) * 128]), w=[wres, ("W", 3)])
                qres = pfx + "qk%d" % sl
                for t in range(5):
                    o, n = TILES[t]
                    bq = bank()
                    bk = bank()
                    if t < 4 or do_ctx_queries:
                        S.pe(MM(P[:, bq, 0:n], [(wq[:, kc, :], hb[:, kc, o:o + n]) for kc in range(KC)]), r=[wres, ("W", 3), ("h", t)], w=[PSR(bq)])
                        S.act(L("activation", out=qk[sl][:, 0, o:o + n], in_=P[:, bq, 0:n], func=AF.Copy, scale=0.125), r=[PSR(bq)], w=[qres])
                    S.pe(MM(P[:, bk, 0:n], [(wk[:, kc, :], hb[:, kc, o:o + n]) for kc in range(KC)]), r=[wres, ("W", 3), ("h", t)], w=[PSR(bk)])
                    S.dve(L("tensor_copy", out=qk[sl][:, 1, o:o + n], in_=P[:, bk, 0:n]), r=[PSR(bk)], w=[qres])

            def emit_EB(h):
                esl = h % 2
                eres = pfx + "EB%d" % esl
                S.dma_pool(L("dma_start", out=EB[esl], in_=bb_d[l, h]), w=[eres])
                S.act(L("activation", out=EB[esl], in_=EB[esl], func=AF.Exp), r=[eres], w=[eres])

            def stage1(hp, hh, it):
                h = hp * 2 + hh
                hr = slice(hh * 64, hh * 64 + 64)
                esl = h % 2
                eres = pfx + "EB%d" % esl
                sl = hp % 2
                qres = pfx + "qk%d" % sl
                qt, klist, nl, base = it
                nk = len(klist)
                b2 = 2 * (acnt[0] % 2)
                psl = acnt[0] % 4
                acnt[0] += 1
                ps = P[:, b2:b2 + 2, :].rearrange("p b n -> p (b n)")

                def fn(e, ps=ps, klist=klist, qt=qt, sl=sl, hr=hr):
                    ins = None
                    for i, kt in enumerate(klist):
                        ins = e.matmul(ps[:, i * 128:(i + 1) * 128], lhsT=qk[sl][hr, 1, kt * 128:(kt + 1) * 128],
                                       rhs=qk[sl][hr, 0, qt * 128:(qt + 1) * 128], start=True, stop=True)
                    return ins
                S.pe(fn, r=[qres], w=[PSR(b2), PSR(b2 + 1)])
                pres = pfx + "PT%d" % psl
                S.act(L("activation", out=PT[psl][:, 0:nk * 128], in_=ps[:, 0:nk * 128], func=AF.Exp),
                      r=[PSR(b2), PSR(b2 + 1)], w=[pres])
                if nl > 0:
                    S.dve(L("tensor_tensor", out=PT[psl][:, 0:nl * 128], in0=PT[psl][:, 0:nl * 128],
                            in1=EB[esl][:, base:base + nl, :].rearrange("p a b -> p (a b)"), op=ALU.mult),
                          r=[pres, eres], w=[pres])
                return psl

            def stage2(hp, hh, it, psl):
                h = hp * 2 + hh
                qt, klist, nl, base = it
                pres = pfx + "PT%d" % psl
                bo = 4 + (ocnt[0] % 3)
                rsl = ocnt[0] % 4
                ocnt[0] += 1
                po = P[:, bo, 0:65]
                S.pe(MM(po, [(PT[psl][:, i * 128:(i + 1) * 128], vx[:, kt, h, :]) for i, kt in enumerate(klist)]),
                     r=[pres, pfx + "vx", pfx + "vx1"], w=[PSR(bo)])
                S.dve(L("reciprocal", out=rden[rsl], in_=P[:, bo, 64:65]), r=[PSR(bo)], w=[pfx + "rden%d" % rsl])
                S.dve(L("tensor_scalar", out=ytok[:, qt, hh * 64:(hh + 1) * 64], in0=P[:, bo, 0:64], scalar1=rden[rsl][:, 0:1], scalar2=None,
                        op0=ALU.mult),
                      r=[PSR(bo), pfx + "rden%d" % rsl], w=[pfx + "ytok%d_%d" % (qt, hh)])
                if hh == 1 and (qt % 4 == 3 or qt == nqt - 1):
                    qts = list(range(4 * (qt // 4), qt + 1))
                    nq = len(qts)

                    def fnT(e, qts=qts):
                        ins = None
                        for qi, q_ in enumerate(qts):
                            ins = e.transpose(Pb[:, qi * 128:(qi + 1) * 128], ytok[:, q_, :], identb[:])
                        return ins
                    S.pe(fnT, r=[pfx + "ytok%d_%d" % (q_, a) for q_ in qts for a in range(2)] + ["const"], w=["psb"])
                    ysl = ystage_i[0] % 2
                    ystage_i[0] += 1
                    S.dve(L("tensor_copy", out=yst[ysl][:, 0:nq * 128], in_=Pb[:, 0:nq * 128]), r=["psb"], w=[pfx + "yst%d" % ysl])
                    o0 = qts[0] * 128
                    S.dma_sp(L("dma_start", out=yscr[:, 2 + hp, o0:o0 + nq * 128], in_=yst[ysl][:, 0:nq * 128]),
                             r=[pfx + "yst%d" % ysl], w=[("yscr", min(qts[0] // 4, 4))])

            LAG = 3
            pend = []
            emit_proj(0)
            emit_EB(0)
            for hp in range(4):
                for hh in range(2):
                    h = hp * 2 + hh
                    for ii, it in enumerate(items):
                        if ii == 6 and h + 1 < 8:
                            emit_EB(h + 1)
                        if hh == 1 and ii == 10 and hp + 1 < 4:
                            emit_proj(hp + 1)
                        psl = stage1(hp, hh, it)
                        pend.append((hp, hh, it, psl))
                        if len(pend) > LAG:
                            stage2(*pend.pop(0))
            while pend:
                stage2(*pend.pop(0))

        def wout_phase(s, l, tiles):
            pfx = phase("wo", top=True)
            yin = [ralloc([KC, 512], BF16) for _ in range(2)]
            assert ar_top[1] >= RC - TOP_WO
            wsrc = wout_d[l].rearrange("(a p) n -> p a n", p=128)
            wo = []
            for hf in range(2):
                dst = W[:, hf, :].rearrange("p (a b) -> p a b", a=KC)
                S.dma_pool(L("dma_start", out=dst, in_=wsrc[:, :, hf * 512:(hf + 1) * 512]), w=[("W", hf)])
                wo.append(dst)
            for t in tiles:
                o, n = TILES[t]
                col = colof(s, t)
                sl = t % 2
                S.dma_sp(L("dma_start", out=yin[sl][:, :, 0:n], in_=yscr[:, :, o:o + n]), r=[("yscr", t)], w=[pfx + "yin%d" % sl])
                for c in range(KC):
                    b = bank()
                    S.pe(MM(P[:, b, 0:n], [(wo[c // 4][:, k, (c % 4) * 128:(c % 4 + 1) * 128], yin[sl][:, k, 0:n]) for k in range(KC)]),
                         r=[("W", c // 4), pfx + "yin%d" % sl], w=[PSR(b)])
                    S.dve(L("scalar_tensor_tensor", out=xT[:, c, o:o + n], in0=P[:, b, 0:n], scalar=G1(l, c, col), in1=xT[:, c, o:o + n],
                            op0=ALU.mult, op1=ALU.add), r=[PSR(b), "mod%d" % l, ("x", t)], w=[("x", t)])

        NG = 8

        def mlp_prefetch(l, g):
            sa = 2 * ((g + 1) % 2)
            s1, s2 = sa, sa + 1
            w1src = w1_d[l].rearrange("(a p) n -> p a n", p=128)
            w2src = w2_d[l].rearrange("(a p) n -> p a n", p=128)
            w1g = W[:, s1, :].rearrange("p (a b) -> p a b", a=KC)
            w2g = W[:, s2, :].rearrange("p (a b) -> p a b", a=4)
            S.dma_pool(L("dma_start", out=w1g, in_=w1src[:, :, g * 512:(g + 1) * 512]), w=[("W", s1)])
            S.dma_pool(L("dma_start", out=w2g, in_=w2src[:, g * 4:(g + 1) * 4, :]), w=[("W", s2)])
            return w1g, w2g, s1, s2

        def mlp_phase(s, l, tiles, pre):
            pfx = phase("mlp", fence=False, top=True, top_base=RC - TOP_WO)
            hid = [ralloc([4, 512], BF16) for _ in range(2)]
            rl = [ralloc([512], F32) for _ in range(3)]
            cnt = {"h": 0, "r": 0}
            wts = {0: pre}
            if deferred_mod["gen"] == "start":
                deferred_mod["gen"] = mod_steps(1, 1, 256)

            def stageA(g, t):
                w1g, w2g, s1, s2 = wts[g]
                o, n = TILES[t]
                hsl = cnt["h"] % 2
                cnt["h"] += 1
                hres = pfx + "hid%d" % hsl
                for j in range(4):
                    b = bank()
                    S.pe(MM(P[:, b, 0:n], [(w1g[:, kc, j * 128:(j + 1) * 128], hb[:, kc, o:o + n]) for kc in range(KC)]),
                         r=[("W", s1), ("h", t)], w=[PSR(b)])
                    rs_ = cnt["r"] % 3
                    cnt["r"] += 1
                    S.act(L("activation", out=rl[rs_][:, 0:n], in_=P[:, b, 0:n], func=AF.Relu), r=[PSR(b)], w=[pfx + "rl%d" % rs_])
                    S.act(L("activation", out=hid[hsl][:, j, 0:n], in_=rl[rs_][:, 0:n], func=AF.Square),
                          r=[pfx + "rl%d" % rs_], w=[hres])
                return hsl

            def stageB(g, t, hsl):
                w1g, w2g, s1, s2 = wts[g]
                o, n = TILES[t]
                col = colof(s, t)
                hres = pfx + "hid%d" % hsl
                for c in range(KC):
                    b = bank()
                    S.pe(MM(P[:, b, 0:n], [(w2g[:, j, c * 128:(c + 1) * 128], hid[hsl][:, j, 0:n]) for j in range(4)]),
                         r=[("W", s2), hres], w=[PSR(b)])
                    S.dve(L("scalar_tensor_tensor", out=xT[:, c, o:o + n], in0=P[:, b, 0:n], scalar=G2(l, c, col), in1=xT[:, c, o:o + n],
                            op0=ALU.mult, op1=ALU.add), r=[PSR(b), "mod%d" % l, ("x", t)], w=[("x", t)])

            pend = []
            for g in range(NG):
                for ti, t in enumerate(tiles):
                    hsl = stageA(g, t)
                    pend.append((g, t, hsl))
                    if len(pend) > 1:
                        stageB(*pend.pop(0))
                    if ti == 0 and g + 1 < NG:
                        wts[g + 1] = mlp_prefetch(l, g + 1)
                    if deferred_mod["gen"] is not None:
                        try:
                            next(deferred_mod["gen"])
                        except StopIteration:
                            deferred_mod["gen"] = None
            while pend:
                stageB(*pend.pop(0))
            if deferred_mod["gen"] is not None:
                for _ in deferred_mod["gen"]:
                    pass
                deferred_mod["gen"] = None

        def final_phase(s):
            pfx = phase("fin")
            sq = ralloc([KC, 512], BF16)
            sd = ralloc([512], F32)
            rstd = ralloc([512], F32)
            tmp = ralloc([KC, 512], F32)
            xo = [ralloc([1024], F32) for _ in range(2)]
            oc = 0
            for t in range(4):
                o, n = TILES[t]
                rms_stats(pfx, t, sq, sd, rstd)
                S.dve(L("tensor_tensor", out=tmp[:, :, 0:n], in0=xT[:, :, o:o + n],
                        in1=rstd[:, 0:n].unsqueeze(1).to_broadcast([128, KC, n]), op=ALU.mult),
                      r=[("x", t), pfx + "rstd"], w=[pfx + "tmp"])
                for kc in range(KC):
                    S.act(L("activation", out=tmp[:, kc, 0:n], in_=tmp[:, kc, 0:n], func=AF.Copy, scale=cpk[:, CP_FG + kc:CP_FG + kc + 1]),
                          r=[pfx + "tmp", "cpk"], w=[pfx + "tmp"])
                for j in range(4):
                    xs = oc % 2
                    oc += 1
                    for half in range(2):
                        b = bank()

                        def fn(e, b=b, half=half, j=j):
                            ins = None
                            for q in range(4):
                                kc = half * 4 + q
                                ins = e.transpose(P[:, b, q * 128:(q + 1) * 128], tmp[:, kc, j * 128:(j + 1) * 128], identf)
                            return ins
                        S.pe(fn, r=[pfx + "tmp", "cpk"], w=[PSR(b)])
                        if half == 0:
                            S.dve(L("tensor_copy", out=xo[xs][:, 0:512], in_=P[:, b, :]), r=[PSR(b)], w=[pfx + "xo%d" % xs])
                        else:
                            S.act(L("activation", out=xo[xs][:, 512:1024], in_=P[:, b, :], func=AF.Copy), r=[PSR(b)], w=[pfx + "xo%d" % xs])
                    S.dma_sp(L("dma_start", out=out_d[s, o + j * 128:o + (j + 1) * 128, :], in_=xo[xs]), r=[pfx + "xo%d" % xs])

        for s in range(nseq):
            load_x(s)
            if debug and s == 0:
                dump("x0", xT[:], [("x", t) for t in range(5)])
            for l in range(nlayers):
                last = (l == DEPTH - 1)
                all5 = list(range(5))
                otiles = [0, 1, 2, 3] if last else all5
                norm_phase(s, l, 1, all5, fence=not (l > 0 and s > 0))
                if debug and s == 0 and l == 0:
                    dump("h", hb[:], [("h", t) for t in range(5)])
                if stop_after == "norm1":
                    break
                conv_phase(s, l, otiles)
                if stop_after == "conv":
                    dump_y()
                    break
                lru_phase(s, l, otiles)
                if stop_after == "lru":
                    dump_y()
                    break
                attn_phase(s, l, do_ctx_queries=not last)
                if debug and s == 0 and l == 0:
                    dump_y()
                if stop_after == "mix":
                    break
                pre = mlp_prefetch(l, 0)
                wout_phase(s, l, otiles)
                if debug and s == 0 and l == 0:
                    dump("x1", xT[:], [("x", t) for t in range(5)])
                norm_phase(s, l, 2, otiles, fence=False)
                if s == 0 and l == 0 and nlayers > 1:
                    deferred_mod["gen"] = "start"
                mlp_phase(s, l, otiles, pre)
                if debug and s == 0 and l == 0:
                    dump("x2", xT[:], [("x", t) for t in range(5)])
            final_phase(s)
        S.run(st)
    return nc, dbg


def fm(v, n):
    return np.ascontiguousarray(np.asarray(v, np.float32).reshape(n, 128).T)


def make_in_maps(inp):
    g = {k: np.asarray(v) for k, v in inp.items()}
    small = np.zeros((DEPTH, 128, NS), np.float32)
    bdw = np.zeros((DEPTH, 128, 8, 128), np.float32)
    bb = np.zeros((DEPTH, 8, 128, NB, 128), np.float32)
    dgw = np.zeros((DEPTH, 128, 62, 128), np.float32)
    dglw = np.zeros((DEPTH, 128, 16, 128), np.float32)
    pidx = np.arange(128)
    for l in range(DEPTH):
        sm = small[l]
        sm[:, SM["n1g"]:SM["n1g"] + 8] = fm(g["norm1_g"][l], 8)
        sm[:, SM["n2g"]:SM["n2g"] + 8] = fm(g["norm2_g"][l], 8)
        cw = g["conv_w"][l]
        for c in range(2):
            sm[:, SM["cw"] + c * 31:SM["cw"] + (c + 1) * 31] = cw[:, c * 128:(c + 1) * 128].T
        sm[:, SM["cb"]:SM["cb"] + 2] = fm(g["conv_b"][l], 2)
        sm[:, SM["lng"]:SM["lng"] + 2] = fm(g["conv_ln_g"][l], 2)
        sm[:, SM["lnb"]:SM["lnb"] + 2] = fm(g["conv_ln_b"][l], 2)
        for d in range(2):
            for c in range(2):
                pi = d * 2 + c
                sm[:, SM["lcw"] + pi * 4:SM["lcw"] + pi * 4 + 4] = g["lru_conv_w"][l, d][:, c * 128:(c + 1) * 128].T
                sm[:, SM["lcb"] + pi] = g["lru_conv_b"][l, d][c * 128:(c + 1) * 128]
                sm[:, SM["lbx"] + pi] = g["lru_bx"][l, d][c * 128:(c + 1) * 128]
                sm[:, SM["lba"] + pi] = g["lru_ba"][l, d][c * 128:(c + 1) * 128]
                sm[:, SM["lam"] + pi] = g["lru_lambda"][l, d][c * 128:(c + 1) * 128]
                for xa, nm in enumerate(("lru_wx", "lru_wa")):
                    wblk = g[nm][l, d]
                    i = (d * 2 + xa) * 2 + c
                    bdw[l, 0:64, i, 0:64] = wblk[2 * c]
                    bdw[l, 64:128, i, 64:128] = wblk[2 * c + 1]
        sm[:, SM["adab"]:SM["adab"] + 48] = fm(g["ada_b"][l], 48)
        dgw[l, pidx, :, pidx] = sm[:, SM["cw"]:SM["cw"] + 62]
        dglw[l, pidx, :, pidx] = sm[:, SM["lcw"]:SM["lcw"] + 16]
        bb[l] = build_bias_blocks(g["na_rpb"][l])
    maps = []
    for i in range(NCORES):
        cpack = np.zeros((128, NCP), np.float32)
        cpack[:, CP_ID:CP_ID + 128] = np.eye(128, dtype=np.float32)
        cpack[:, CP_FG:CP_FG + 8] = fm(g["final_g"], 8)
        cvecs = [g["c"][2 * i], g["c"][2 * i + 1], g["c_ctx"]]
        ct = np.stack([fm(v, 8) for v in cvecs], axis=-1)
        cpack[:, CP_CT:CP_CT + 24] = ct.reshape(128, 24)
        maps.append({
            "x": np.ascontiguousarray(g["x"][2 * i:2 * i + 2]),
            "ctx": np.ascontiguousarray(g["ctx"][2 * i:2 * i + 2]),
            "cpack": cpack, "small": small, "ada_w": g["ada_w"], "w_in": g["w_in"], "w_out": g["w_out"],
            "w1": g["mlp_w1"], "w2": g["mlp_w2"], "bdw": bdw, "bb": bb, "dgw": dgw, "dglw": dglw,
        })
    return maps


_NC_CACHE = {}


def kernel(**inputs):
    if "nc" not in _NC_CACHE:
        _NC_CACHE["nc"] = build_program()[0]
    nc = _NC_CACHE["nc"]
    maps = make_in_maps(inputs)
    res = run_bass_kernel_spmd(nc, maps, core_ids=list(range(NCORES)))
    out = np.concatenate([np.asarray(r["out"]) for r in res.results], axis=0)
    return out.astype(np.float32)
```
